# Optimizing a Trainium2 kernel written in Bass

```python
import math
import jax
import jax.numpy as jnp
from jax import lax
import numpy as np


D_MODEL = 2048
BATCH = 4
SEQ = 8192
DEPTH = 2

HEAD_DIM = 128
SB_WIDTH = 3 * D_MODEL // 4
SB_HEADS = SB_WIDTH // HEAD_DIM
SB_QBLOCK = 128
SSM_WIDTH = D_MODEL // 4
SSM_GROUP = 16
SSM_GROUPS = SSM_WIDTH // SSM_GROUP
SSM_STATE = 64
EVEN_MIX = SB_WIDTH + SSM_WIDTH
EVEN_IN = 4 * SB_WIDTH + 2 * SSM_WIDTH
MOBA_WIDTH = D_MODEL
MOBA_HEADS = MOBA_WIDTH // HEAD_DIM
MOBA_BLOCK = 256
MOBA_TOPK = 3
MOBA_QCHUNK = 16
ODD_IN = 4 * MOBA_WIDTH
N_EVEN = (DEPTH + 1) // 2
N_ODD = DEPTH // 2
DN_ALPHA = (2 * DEPTH) ** 0.25
DN_BETA = (8 * DEPTH) ** -0.25
LN_EPS = 1e-5
NEG = -1e30

kernel_name = "hybrid_stickbreak_s5_moba_deepnorm"


def _layer_norm(x, g, b):
    xf = x.astype(jnp.float32)
    mu = jnp.mean(xf, axis=-1, keepdims=True)
    var = jnp.mean(jnp.square(xf - mu), axis=-1, keepdims=True)
    y = (xf - mu) * lax.rsqrt(var + LN_EPS) * g.astype(jnp.float32) + b.astype(jnp.float32)
    return y.astype(x.dtype)


def _split_heads(t, n_heads):
    b, s, _ = t.shape
    return t.reshape(b, s, n_heads, HEAD_DIM).transpose(0, 2, 1, 3)


def _merge_heads(t):
    b, h, s, d = t.shape
    return t.transpose(0, 2, 1, 3).reshape(b, s, h * d)


def stick_breaking_attention(q, k, v):
    _, _, s, d = q.shape
    scale = d ** -0.5
    outs = []
    for blk in range(s // SB_QBLOCK):
        start = blk * SB_QBLOCK
        end = start + SB_QBLOCK
        z = jnp.einsum('bhqd,bhkd->bhqk', q[:, :, start:end], k[:, :, :end]).astype(jnp.float32) * scale
        t_pos = start + jnp.arange(SB_QBLOCK)
        s_pos = jnp.arange(end)
        past = s_pos[None, :] < t_pos[:, None]
        log_beta = jax.nn.log_sigmoid(z)
        log_keep = jnp.where(past, log_beta - z, 0.0)
        later = lax.cumsum(log_keep, axis=3, reverse=True) - log_keep
        w = jnp.where(past, jnp.exp(log_beta + later), 0.0)
        outs.append(jnp.einsum('bhqk,bhkd->bhqd', w.astype(v.dtype), v[:, :, :end]))
    return jnp.concatenate(outs, axis=2)


def _linear_recurrence(left, right):
    a_l, b_l = left
    a_r, b_r = right
    return a_r * a_l, a_r * b_l + b_r


def s5_ssm(u, a_re, a_im, log_dt, b_re, b_im, c_re, c_im, d_skip, w_glu):
    bsz, s, _ = u.shape
    uf = u.astype(jnp.float32).reshape(bsz, s, SSM_GROUPS, SSM_GROUP)
    lam = lax.complex(a_re.astype(jnp.float32), a_im.astype(jnp.float32))
    dt = jnp.exp(log_dt.astype(jnp.float32))[:, None]
    a_bar = jnp.exp(lam * dt)
    b_c = lax.complex(b_re.astype(jnp.float32), b_im.astype(jnp.float32))
    b_bar = ((a_bar - 1.0) / lam)[..., None] * b_c
    bu = jnp.einsum('bsgi,gpi->bsgp', uf.astype(jnp.complex64), b_bar)
    a_seq = jnp.broadcast_to(a_bar, bu.shape)
    _, h = lax.associative_scan(_linear_recurrence, (a_seq, bu), axis=1)
    c_c = lax.complex(c_re.astype(jnp.float32), c_im.astype(jnp.float32))
    y = jnp.real(jnp.einsum('bsgp,gip->bsgi', h, c_c))
    y = y + d_skip.astype(jnp.float32).reshape(SSM_GROUPS, SSM_GROUP) * uf
    y = jax.nn.gelu(y.reshape(bsz, s, SSM_WIDTH))
    y = y * jax.nn.sigmoid(y @ w_glu.astype(jnp.float32))
    return y.astype(u.dtype)


def moba_attention(q, k, v):
    bsz, h, s, d = q.shape
    nb = -(-s // MOBA_BLOCK)
    pad = nb * MOBA_BLOCK - s
    kb = jnp.pad(k, ((0, 0), (0, 0), (0, pad), (0, 0))).reshape(bsz, h, nb, MOBA_BLOCK, d)
    vb = jnp.pad(v, ((0, 0), (0, 0), (0, pad), (0, 0))).reshape(bsz, h, nb, MOBA_BLOCK, d)
    k_mean = jnp.mean(kb.astype(jnp.float32), axis=3)
    gate = jnp.einsum('bhsd,bhnd->bhsn', q.astype(jnp.float32), k_mean)
    q_block = jnp.arange(s) // MOBA_BLOCK
    fully_past = jnp.arange(nb)[None, :] < q_block[:, None]
    gate = jnp.where(fully_past, gate, NEG)
    n_sel = min(MOBA_TOPK, nb)
    top_val, top_idx = lax.top_k(gate, n_sel)
    sel_valid = top_val > 0.5 * NEG
    n_chunks = s // MOBA_QCHUNK
    scale = d ** -0.5
    gather = jax.vmap(jax.vmap(lambda blocks, ids: blocks[ids]))

    def to_chunks(t):
        return jnp.moveaxis(t.reshape(bsz, h, n_chunks, MOBA_QCHUNK, t.shape[-1]), 2, 0)

    def chunk(args):
        qc, ic, vc, ci = args
        k_sel = gather(kb, ic)
        v_sel = gather(vb, ic)
        s_sel = jnp.einsum('bhqd,bhqnkd->bhqnk', qc, k_sel).astype(jnp.float32) * scale
        s_sel = jnp.where(vc[..., None], s_sel, -jnp.inf).reshape(bsz, h, MOBA_QCHUNK, n_sel * MOBA_BLOCK)
        own = (ci * MOBA_QCHUNK) // MOBA_BLOCK
        k_own = lax.dynamic_index_in_dim(kb, own, axis=2, keepdims=False)
        v_own = lax.dynamic_index_in_dim(vb, own, axis=2, keepdims=False)
        s_own = jnp.einsum('bhqd,bhkd->bhqk', qc, k_own).astype(jnp.float32) * scale
        q_pos = ci * MOBA_QCHUNK + jnp.arange(MOBA_QCHUNK)
        k_pos = own * MOBA_BLOCK + jnp.arange(MOBA_BLOCK)
        s_own = jnp.where(k_pos[None, :] <= q_pos[:, None], s_own, -jnp.inf)
        p = jax.nn.softmax(jnp.concatenate([s_sel, s_own], axis=-1), axis=-1).astype(v.dtype)
        p_sel = p[..., :n_sel * MOBA_BLOCK].reshape(bsz, h, MOBA_QCHUNK, n_sel, MOBA_BLOCK)
        p_own = p[..., n_sel * MOBA_BLOCK:]
        return (jnp.einsum('bhqnk,bhqnkd->bhqd', p_sel, v_sel)
                + jnp.einsum('bhqk,bhkd->bhqd', p_own, v_own))

    out = lax.map(chunk, (to_chunks(q), to_chunks(top_idx), to_chunks(sel_valid), jnp.arange(n_chunks)))
    return jnp.moveaxis(out, 0, 2).reshape(bsz, h, s, d)


def even_mixer(h, w_in, w_out, a_re, a_im, log_dt, b_re, b_im, c_re, c_im, d_skip, w_glu):
    proj = h @ w_in
    q, k, v, g_sb, u, g_ssm = jnp.split(
        proj, [SB_WIDTH, 2 * SB_WIDTH, 3 * SB_WIDTH, 4 * SB_WIDTH, 4 * SB_WIDTH + SSM_WIDTH], axis=-1)
    o_sb = _merge_heads(stick_breaking_attention(
        _split_heads(q, SB_HEADS), _split_heads(k, SB_HEADS), _split_heads(v, SB_HEADS)))
    o_sb = o_sb * jax.nn.silu(g_sb)
    o_ssm = s5_ssm(u, a_re, a_im, log_dt, b_re, b_im, c_re, c_im, d_skip, w_glu) * jax.nn.silu(g_ssm)
    return jnp.concatenate([o_sb, o_ssm.astype(o_sb.dtype)], axis=-1) @ w_out


def odd_mixer(h, w_in, w_out):
    q, k, v, g = jnp.split(h @ w_in, 4, axis=-1)
    o = _merge_heads(moba_attention(
        _split_heads(q, MOBA_HEADS), _split_heads(k, MOBA_HEADS), _split_heads(v, MOBA_HEADS)))
    return (o * jax.nn.silu(g)) @ w_out


def setup_inputs(seed: int = 0) -> dict:
    key = jax.random.key(seed)
    ks = jax.random.split(key, 20)
    f32 = jnp.float32

    def nrm(k, shape, s):
        return jax.random.normal(k, shape, f32) * s

    ssm_a_im = (math.pi * jnp.arange(SSM_STATE, dtype=f32))[None, None, :] + nrm(ks[7], (N_EVEN, SSM_GROUPS, SSM_STATE), 0.01)
    return {
        'x': nrm(ks[0], (BATCH, SEQ, D_MODEL), 1.0),
        'c': nrm(ks[1], (BATCH, D_MODEL), 1.0),
        'ada_w': nrm(ks[2], (DEPTH, D_MODEL, 3 * D_MODEL), 0.5 * D_MODEL ** -0.5),
        'ada_b': nrm(ks[3], (DEPTH, 3 * D_MODEL), 0.01),
        'ln_g': 1.0 + nrm(ks[4], (DEPTH, D_MODEL), 0.02),
        'ln_b': nrm(ks[5], (DEPTH, D_MODEL), 0.02),
        'even_w_in': nrm(ks[6], (N_EVEN, D_MODEL, EVEN_IN), D_MODEL ** -0.5),
        'even_w_out': nrm(ks[8], (N_EVEN, EVEN_MIX, D_MODEL), EVEN_MIX ** -0.5 * DN_BETA),
        'ssm_a_re': -0.5 + nrm(ks[9], (N_EVEN, SSM_GROUPS, SSM_STATE), 0.01),
        'ssm_a_im': ssm_a_im,
        'ssm_log_dt': jax.random.uniform(ks[10], (N_EVEN, SSM_GROUPS), f32, math.log(1e-3), math.log(1e-1)),
        'ssm_b_re': nrm(ks[11], (N_EVEN, SSM_GROUPS, SSM_STATE, SSM_GROUP), (2 * SSM_GROUP) ** -0.5),
        'ssm_b_im': nrm(ks[12], (N_EVEN, SSM_GROUPS, SSM_STATE, SSM_GROUP), (2 * SSM_GROUP) ** -0.5),
        'ssm_c_re': nrm(ks[13], (N_EVEN, SSM_GROUPS, SSM_GROUP, SSM_STATE), SSM_STATE ** -0.5),
        'ssm_c_im': nrm(ks[14], (N_EVEN, SSM_GROUPS, SSM_GROUP, SSM_STATE), SSM_STATE ** -0.5),
        'ssm_d': nrm(ks[15], (N_EVEN, SSM_WIDTH), 1.0),
        'ssm_w_glu': nrm(ks[16], (N_EVEN, SSM_WIDTH, SSM_WIDTH), SSM_WIDTH ** -0.5),
        'odd_w_in': nrm(ks[17], (N_ODD, D_MODEL, ODD_IN), D_MODEL ** -0.5),
        'odd_w_out': nrm(ks[18], (N_ODD, MOBA_WIDTH, D_MODEL), MOBA_WIDTH ** -0.5 * DN_BETA),
    }


def reference(x, c, ada_w, ada_b, ln_g, ln_b, even_w_in, even_w_out, ssm_a_re, ssm_a_im, ssm_log_dt,
              ssm_b_re, ssm_b_im, ssm_c_re, ssm_c_im, ssm_d, ssm_w_glu, odd_w_in, odd_w_out):
    cond = jax.nn.silu(c)
    for layer in range(DEPTH):
        shift, scale, gate = jnp.split(cond @ ada_w[layer] + ada_b[layer], 3, axis=-1)
        h = x * (1.0 + scale[:, None, :]) + shift[:, None, :]
        i = layer // 2
        if layer % 2 == 0:
            y = even_mixer(h, even_w_in[i], even_w_out[i], ssm_a_re[i], ssm_a_im[i], ssm_log_dt[i],
                           ssm_b_re[i], ssm_b_im[i], ssm_c_re[i], ssm_c_im[i], ssm_d[i], ssm_w_glu[i])
        else:
            y = odd_mixer(h, odd_w_in[i], odd_w_out[i])
        x = _layer_norm(DN_ALPHA * x + (1.0 + gate[:, None, :]) * y.astype(x.dtype), ln_g[layer], ln_b[layer])
    return x
```

```python
import contextlib
import numpy as np
import concourse.bass as bass
import concourse.mybir as mybir
from concourse.bass_utils import run_bass_kernel_spmd

F32 = mybir.dt.float32
BF16 = mybir.dt.bfloat16
AF = mybir.ActivationFunctionType
ALU = mybir.AluOpType
AX = mybir.AxisListType


class Buf:
    __slots__ = ("name", "w", "r", "frozen", "excl")

    def __init__(self, name, excl=False):
        self.name = name
        self.w = None
        self.r = {}
        self.frozen = False
        self.excl = excl


class Op:
    __slots__ = ("eng", "fn", "waits", "sig", "count", "idx", "dsem", "dval", "isdma")


ENGS = ("pe", "act", "dve", "pool", "sp")
NDMASEM = 24


class Prog:
    def __init__(self, nc):
        self.nc = nc
        self.ops = {e: [] for e in ENGS}
        self.waited = {e: {} for e in ENGS}
        self.dma_last = [None] * NDMASEM
        self.dma_cnt = [0] * NDMASEM
        self.dma_rr = 0
        self.nbuf = 0
        self.pending = {e: [] for e in ENGS}

    def barrier(self):
        deps = [self.ops[e][-1] for e in ENGS if self.ops[e]]
        deps += [d for d in self.dma_last if d is not None]
        for e in ENGS:
            self.pending[e] = list(deps)

    def buf(self, name=None, excl=False):
        self.nbuf += 1
        return Buf(name or f"b{self.nbuf}", excl)

    def _dep(self, op, d):
        if d is None or d is op:
            return
        E = op.eng
        wt = self.waited[E]
        if d.isdma:
            key = ("d", d.dsem)
            if wt.get(key, 0) >= d.dval:
                return
            wt[key] = d.dval
            op.waits.append(d)
        else:
            X = d.eng
            if X == E and E == "pe":
                return
            if wt.get(X, -1) >= d.idx:
                return
            wt[X] = d.idx
            d.sig = True
            op.waits.append(d)

    def _record(self, eng, fn, reads, writes, isdma):
        xr = [b for b in reads if b.excl]
        if xr:
            reads = [b for b in reads if not b.excl]
            writes = list(writes) + [b for b in xr if b not in writes]
        op = Op()
        op.eng = eng
        op.fn = fn
        op.waits = []
        op.sig = False
        op.count = 0
        op.isdma = isdma
        op.dsem = None
        op.dval = 0
        op.idx = len(self.ops[eng])
        if self.pending[eng]:
            for d in self.pending[eng]:
                self._dep(op, d)
            self.pending[eng] = []
        for b in reads:
            self._dep(op, b.w)
        for b in writes:
            self._dep(op, b.w)
            for r in b.r.values():
                self._dep(op, r)
        if isdma:
            k = self.dma_rr
            self.dma_rr = (k + 1) % NDMASEM
            self._dep(op, self.dma_last[k])
            self.dma_cnt[k] += 16
            op.dsem = k
            op.dval = self.dma_cnt[k]
            self.dma_last[k] = op
        for b in reads:
            if not b.frozen:
                key = (eng, op.idx) if isdma else eng
                b.r[key] = op
        for b in writes:
            b.w = op
            b.r = {}
        self.ops[eng].append(op)
        return op

    def op(self, eng, fn, reads=(), writes=()):
        return self._record(eng, fn, reads, writes, False)

    def dma(self, eng, fn, reads=(), writes=()):
        return self._record(eng, fn, reads, writes, True)

    def emit(self, final_waits=()):
        nc = self.nc
        lasts = []
        for e in ENGS:
            if e != "sp" and self.ops[e]:
                lo = self.ops[e][-1]
                if not lo.isdma:
                    lo.sig = True
                    lasts.append(lo)
        for e in ENGS:
            c = 0
            for op in self.ops[e]:
                if op.sig and not op.isdma:
                    c += 1
                    op.count = c
        with contextlib.ExitStack() as st:
            esem = {e: st.enter_context(nc.semaphore(f"s_{e}")) for e in ENGS}
            dsem = [st.enter_context(nc.semaphore(f"s_d{k}")) for k in range(NDMASEM)]
            block = st.enter_context(nc.Block())

            def run(e, eng):
                for op in self.ops[e]:
                    for d in op.waits:
                        if d.isdma:
                            eng.wait_ge(dsem[d.dsem], d.dval)
                        else:
                            eng.wait_ge(esem[d.eng], d.count)
                    ins = op.fn(eng)
                    if op.isdma:
                        ins.then_inc(dsem[op.dsem], 16)
                    elif op.sig:
                        ins.then_inc(esem[e], 1)
                if e == "sp":
                    for lo in lasts:
                        eng.wait_ge(esem[lo.eng], lo.count)
                    for k in range(NDMASEM):
                        if self.dma_cnt[k]:
                            eng.wait_ge(dsem[k], self.dma_cnt[k])

            @block.tensor
            def _(eng):
                run("pe", eng)

            @block.scalar
            def _(eng):
                run("act", eng)

            @block.vector
            def _(eng):
                run("dve", eng)

            @block.gpsimd
            def _(eng):
                run("pool", eng)

            @block.sync
            def _(eng):
                run("sp", eng)

    def stats(self):
        return {e: len(self.ops[e]) for e in ENGS}


import os
import numpy as np
import ml_dtypes

SB_BASE = 16512
SB_SIZE = 204800
S = 8192
D = 2048
NT = S // 512
BIG = 30000.0
NCONST = 7 * 128
NROT = int(os.environ.get('NROT', 2))


class Tile:
    def __init__(self, ctx, name, off, shape, dtype):
        off = (off + 31) // 32 * 32
        self.t = ctx.nc.alloc_sbuf_tensor_at(f"{name}_{ctx.uid()}", list(shape), dtype, offset=SB_BASE + off)
        self.b = ctx.P.buf(name)
        esz = 2 if dtype == BF16 else 4
        n = 1
        for s in shape[1:]:
            n *= s
        self.end = off + n * esz
        assert self.end <= SB_SIZE, (name, self.end)

    def __getitem__(self, k):
        return self.t[k]


class Ctx:
    def __init__(self, nc):
        self.nc = nc
        self.P = Prog(nc)
        self._uid = 0
        self.arena = nc.alloc_sbuf_tensor("arena", [128, SB_SIZE], mybir.dt.uint8)
        self.ps = []
        self.psb = []
        for k in range(8):
            t = nc.alloc_psum_tensor(f"psum{k}", [128, 512], F32)
            self.ps.append(t)
            self.psb.append(self.P.buf(f"psum{k}", excl=True))

    def uid(self):
        self._uid += 1
        return self._uid

    def tile(self, name, off, shape, dtype):
        return Tile(self, name, off, shape, dtype)

    def psbf(self, k):
        return self.ps[k][:].bitcast(BF16)


def load_consts(ctx, consts_bf, off):
    P = ctx.P
    t = ctx.tile("consts", off, [128, NCONST], BF16)
    P.dma("sp", lambda e: e.dma_start(out=t[:], in_=consts_bf), writes=[t.b])
    t.b.frozen = True
    c = {}
    for i, n in enumerate(["ident", "uinc", "lstr", "sbmask", "mbmask", "ones", "zeros"]):
        c[n] = t[:, i * 128:(i + 1) * 128]
    c["buf"] = t.b
    c["end"] = t.end
    return c


def make_consts_np():
    j = np.arange(128)[:, None]
    s = np.arange(128)[None, :]
    ident = (j == s).astype(np.float32)
    uinc = (j >= s).astype(np.float32)
    lstr = (j < s).astype(np.float32)
    sbmask = np.where(j < s, 0.0, -BIG)
    mbmask = np.where(j <= s, 0.0, -BIG)
    ones = np.ones((128, 128), np.float32)
    zeros = np.zeros((128, 128), np.float32)
    return np.concatenate([ident, uinc, lstr, sbmask, mbmask, ones, zeros], axis=1).astype(ml_dtypes.bfloat16)


def phase_ada(ctx, c16, ada_w_l, ada_b_l, off, adaT, scr_gate, gate_bc):
    nc, P = ctx.nc, ctx.P
    condT = ctx.tile("condT", off, [128, 16], F32)
    abT = ctx.tile("abT", condT.end, [128, 48], F32)
    slab = [ctx.tile(f"slab{i}", abT.end + i * 32768, [128, 16, 512], F32) for i in range(2)]
    P.dma("sp", lambda e: e.dma_start(out=condT[:], in_=c16.rearrange("k p -> p k"), allow_slow_non_contiguous=True),
          writes=[condT.b])
    P.dma("sp", lambda e: e.dma_start(out=abT[:], in_=ada_b_l.rearrange("k p -> p k"), allow_slow_non_contiguous=True),
          writes=[abT.b])
    P.op("act", lambda e: e.activation(out=condT[:], in_=condT[:], func=AF.Silu), reads=[condT.b], writes=[condT.b])
    pb = 7
    for jg in range(12):
        sl = slab[jg % 2]
        P.dma("sp" if jg % 2 == 0 else "act",
              lambda e, sl=sl, jg=jg: e.dma_start(
                  out=sl[:], in_=ada_w_l[:, jg * 512:(jg + 1) * 512].rearrange("(k p) f -> p k f", p=128)),
              writes=[sl.b])
        for jj in range(4):
            j = jg * 4 + jj
            for kc in range(16):
                P.op("pe", lambda e, sl=sl, jj=jj, kc=kc, j=j: e.matmul(
                    ctx.ps[pb][:, j:j + 1], lhsT=sl[:, kc, jj * 128:(jj + 1) * 128], rhs=condT[:, kc:kc + 1],
                    start=(kc == 0), stop=(kc == 15)),
                    reads=[sl.b, condT.b], writes=[ctx.psb[pb]])
    P.op("dve", lambda e: e.tensor_tensor(out=adaT[:], in0=ctx.ps[pb][:, 0:48], in1=abT[:], op=ALU.add),
         reads=[ctx.psb[pb], abT.b], writes=[adaT.b])
    P.op("dve", lambda e: e.tensor_scalar(out=adaT[:, 16:48], in0=adaT[:, 16:48], scalar1=1.0, scalar2=None, op0=ALU.add),
         reads=[adaT.b], writes=[adaT.b])
    gb = P.buf("scr_gate")
    P.dma("sp", lambda e: e.dma_start(out=scr_gate.rearrange("k p -> p k"), in_=adaT[:, 32:48],
                                      allow_slow_non_contiguous=True), reads=[adaT.b], writes=[gb])
    P.dma("sp", lambda e: e.dma_start(out=gate_bc[:], in_=scr_gate.rearrange("k p -> (k p)").partition_broadcast(128)),
          reads=[gb], writes=[gate_bc.b])


def phase_prepass(ctx, x_b, hT_d, adaT, consts, off, hbuf=None):
    nc, P = ctx.nc, ctx.P
    xt = [ctx.tile(f"xt{i}", off + i * 16384, [128, 4, 2048], BF16) for i in range(2)]
    ht = [ctx.tile(f"ht{i}", off + 32768 + i * 16384, [128, 16, 512], BF16) for i in range(2)]
    hb = hbuf or P.buf("hT_d")
    for tt in range(NT):
        x_ = xt[tt % 2]
        h_ = ht[tt % 2]
        P.dma("pool", lambda e, x_=x_, tt=tt: e.dma_start(
            out=x_[:], in_=x_b[tt * 512:(tt + 1) * 512, :].rearrange("(b p) f -> p b f", p=128)), writes=[x_.b])
        for kc in range(16):
            pb = kc % 2
            for blk in range(4):
                P.op("pe", lambda e, x_=x_, kc=kc, blk=blk, pb=pb: e.transpose(
                    ctx.psbf(pb)[:, blk * 128:(blk + 1) * 128], x_[:, blk, kc * 128:(kc + 1) * 128], consts["ident"]),
                    reads=[x_.b, consts["buf"]], writes=[ctx.psb[pb]])
            P.op("act", lambda e, h_=h_, kc=kc, pb=pb: e.activation(
                out=h_[:, kc, :], in_=ctx.psbf(pb)[:, 0:512], func=AF.Identity,
                scale=adaT[:, 16 + kc:17 + kc], bias=adaT[:, kc:kc + 1]),
                reads=[ctx.psb[pb], adaT.b], writes=[h_.b])
        P.dma("sp", lambda e, h_=h_, tt=tt: e.dma_start(out=hT_d[:, :, tt * 512:(tt + 1) * 512], in_=h_[:]),
              reads=[h_.b], writes=[hb])
    return hb


def alloc_head_tiles(ctx, off):
    T = {}
    T["qT"] = ctx.tile("qT", off, [128, S], BF16)
    T["kT"] = ctx.tile("kT", T["qT"].end, [128, S], BF16)
    T["v"] = ctx.tile("v", T["kT"].end, [128, S], BF16)
    T["sgT"] = ctx.tile("sgT", T["v"].end, [128, S], BF16)
    T["W"] = ctx.tile("W", T["sgT"].end, [128, 16, 512], BF16)
    T["ht"] = [ctx.tile(f"hti{i}", T["W"].end + i * 16384, [128, 16, 512], BF16) for i in range(2)]
    T["end"] = T["ht"][1].end
    return T


def inproj_head(ctx, T, w_d, col0, hT_d, hbuf, psbanks=(0, 1, 2, 3), gate_func=AF.Silu):
    P = ctx.P
    W = T["W"]
    P.dma("pool", lambda e: e.dma_start(out=W[:], in_=w_d[:, col0:col0 + 512].rearrange("(k p) f -> p k f", p=128)),
          writes=[W.b])
    nb = 0
    for tt in range(NT):
        ht = T["ht"][tt % 2]
        P.dma("sp", lambda e, ht=ht, tt=tt: e.dma_start(out=ht[:], in_=hT_d[:, :, tt * 512:(tt + 1) * 512]),
              reads=[hbuf], writes=[ht.b])
        ts = slice(tt * 512, (tt + 1) * 512)
        for ci, dst, fn in ((0, T["qT"], AF.Identity), (1, T["kT"], AF.Identity), (3, T["sgT"], gate_func)):
            pb = psbanks[nb % len(psbanks)]
            nb += 1
            for kc in range(16):
                P.op("pe", lambda e, pb=pb, kc=kc, ci=ci, ht=ht: e.matmul(
                    ctx.ps[pb][:, :], lhsT=W[:, kc, ci * 128:(ci + 1) * 128], rhs=ht[:, kc, :],
                    start=(kc == 0), stop=(kc == 15)), reads=[W.b, ht.b], writes=[ctx.psb[pb]])
            P.op("act", lambda e, pb=pb, dst=dst, ts=ts, fn=fn: e.activation(out=dst[:, ts], in_=ctx.ps[pb][:, :], func=fn),
                 reads=[ctx.psb[pb]], writes=[dst.b])
        pb = psbanks[nb % len(psbanks)]
        nb += 1
        for blk in range(4):
            for kc in range(16):
                P.op("pe", lambda e, pb=pb, kc=kc, blk=blk, ht=ht: e.matmul(
                    ctx.ps[pb][:, blk * 128:(blk + 1) * 128], lhsT=ht[:, kc, blk * 128:(blk + 1) * 128],
                    rhs=W[:, kc, 256:384], start=(kc == 0), stop=(kc == 15)),
                    reads=[W.b, ht.b], writes=[ctx.psb[pb]])
        P.op("dve", lambda e, pb=pb, ts=ts: e.tensor_copy(out=T["v"][:, ts], in_=ctx.ps[pb][:, :]),
             reads=[ctx.psb[pb]], writes=[T["v"].b])


def alloc_attn_tmp(ctx, off):
    A = {}
    o = off
    for nm, dt, sz in (("e", F32, 2048), ("sp", BF16, 1024), ("arg", F32, 2048), ("w", BF16, 1024), ("og", BF16, 1024)):
        A[nm] = []
        for i in range(NROT):
            A[nm].append(ctx.tile(f"{nm}{i}", o, [128, 512], dt))
            o += sz
    A["end"] = o
    return A


def sb_attention_head(ctx, T, A, consts, og_d, row0, ogbuf):
    P = ctx.P
    scale = 128 ** -0.5
    qT, kT, v, sgT = T["qT"], T["kT"], T["v"], T["sgT"]
    cb = consts["buf"]
    steps = []
    import os
    for G in range(int(os.environ.get("NG", NT))):
        for kb in range(4 * G + 3, -1, -1):
            steps.append((G, kb))
    steps = steps[int(os.environ.get("SKIP", 0)):int(os.environ.get("NSTEPS", len(steps)))]
    n = len(steps)

    def geom(i):
        G, kb = steps[i]
        diag = kb >= 4 * G
        c0 = (kb - 4 * G) * 128 if diag else 0
        return G, kb, diag, c0, i % NROT, 3 + G % 2, 5 + G % 2

    def Zmm(i):
        G, kb, diag, c0, zb, Cb, Ob = geom(i)
        ks = slice(kb * 128, (kb + 1) * 128)
        q0 = G * 512
        z = ctx.ps[zb]
        if kb == 4 * G + 3 or i == 0:
            for bk in (Cb, Ob):
                P.op("pe", lambda e, bk=bk: e.matmul(ctx.ps[bk][:, 0:512], lhsT=consts["zeros"], rhs=qT[:, q0:q0 + 512],
                                                     start=True, stop=False, skip_group_check=True),
                     reads=[cb, qT.b], writes=[ctx.psb[bk]])
        P.op("pe", lambda e: e.matmul(z[:, c0:512], lhsT=kT[:, ks], rhs=qT[:, q0 + c0:q0 + 512],
                                      start=True, stop=not diag), reads=[kT.b, qT.b], writes=[ctx.psb[zb]])
        if diag:
            P.op("pe", lambda e: e.matmul(z[:, c0:c0 + 128], lhsT=consts["ident"], rhs=consts["sbmask"],
                                          start=False, stop=True), reads=[cb], writes=[ctx.psb[zb]])

    def Zact(i):
        G, kb, diag, c0, zb, Cb, Ob = geom(i)
        e_, sp_ = A["e"][zb], A["sp"][zb]
        X = os.environ.get("X%d" % i, "")
        if "e" not in X:
          P.op("act", lambda e: e.activation(out=e_[:, c0:512], in_=ctx.ps[zb][:, c0:512], func=AF.Exp, scale=scale),
             reads=[ctx.psb[zb]], writes=[e_.b])
        if "l" not in X:
          P.op("act", lambda e: e.activation(out=sp_[:, c0:512], in_=e_[:, c0:512], func=(AF.Identity if os.environ.get("NOLN") else AF.Ln), bias=1.0),
             reads=[e_.b], writes=[sp_.b])
        arg_ = A["arg"][zb]
        if "s" not in X:
          P.op("dve", lambda e: e.tensor_scalar(out=arg_[:, c0:512], in0=ctx.ps[zb][:, c0:512], scalar1=scale, scalar2=None,
                                              op0=ALU.mult), reads=[ctx.psb[zb]], writes=[arg_.b])

    def U(i):
        G, kb, diag, c0, zb, Cb, Ob = geom(i)
        sp_, arg_ = A["sp"][zb], A["arg"][zb]
        C = ctx.ps[Cb]
        P.op("pe", lambda e: e.matmul(C[:, c0:512], lhsT=consts["uinc"], rhs=sp_[:, c0:512],
                                      start=False, stop=False, skip_group_check=True),
             reads=[cb, sp_.b], writes=[ctx.psb[Cb]])
        P.op("dve", lambda e: e.tensor_tensor(out=arg_[:, c0:512], in0=arg_[:, c0:512], in1=C[:, c0:512],
                                              op=ALU.subtract), reads=[arg_.b, ctx.psb[Cb]], writes=[arg_.b])
        if kb > 0:
            P.op("pe", lambda e: e.matmul(C[:, c0:512], lhsT=consts["lstr"], rhs=sp_[:, c0:512],
                                          start=False, stop=False, skip_group_check=True),
                 reads=[cb, sp_.b], writes=[ctx.psb[Cb]])

    def Wact(i):
        G, kb, diag, c0, zb, Cb, Ob = geom(i)
        arg_, w_ = A["arg"][zb], A["w"][zb]
        P.op("act", lambda e: e.activation(out=w_[:, c0:512], in_=arg_[:, c0:512], func=AF.Exp),
             reads=[arg_.b], writes=[w_.b])

    def PV(i):
        G, kb, diag, c0, zb, Cb, Ob = geom(i)
        w_ = A["w"][zb]
        O = ctx.ps[Ob]
        ks = slice(kb * 128, (kb + 1) * 128)
        last = (kb == 0) or (i == n - 1)
        P.op("pe", lambda e: e.matmul(O[:, c0:512], lhsT=v[:, ks], rhs=w_[:, c0:512],
                                      start=False, stop=False, skip_group_check=True),
             reads=[v.b, w_.b], writes=[ctx.psb[Ob]])
        if last:
            og_ = A["og"][G % 2]
            P.op("dve", lambda e: e.tensor_tensor(out=og_[:, :], in0=O[:, :], in1=sgT[:, G * 512:(G + 1) * 512],
                                                  op=ALU.mult), reads=[ctx.psb[Ob], sgT.b], writes=[og_.b])
            P.dma("sp", lambda e: e.dma_start(out=og_d[row0:row0 + 128, G * 512:(G + 1) * 512], in_=og_[:, :]),
                  reads=[og_.b], writes=[ogbuf])

    if os.environ.get("SEQ"):
        dis = os.environ.get("DIS", "")
        for i in range(n):
            Zmm(i)
            if "a" not in dis: Zact(i)
            if "u" not in dis: U(i)
            if "w" not in dis: Wact(i)
            if "p" not in dis or i == n - 1: PV(i)
        return
    Zmm(0)
    Zact(0)
    for i in range(n):
        if i + 1 < n:
            Zmm(i + 1)
        if i >= 1:
            PV(i - 1)
        if i + 1 < n:
            Zact(i + 1)
        U(i)
        Wact(i)
    PV(n - 1)


TWO_PI = 2.0 * np.pi
LT = 512


def ssm_layout_np(a_re, a_im, log_dt, b_re, b_im, c_re, c_im, d_skip, w_glu, own):
    order = np.concatenate([np.arange(16 * own, 16 * own + 16), np.arange(16 * (1 - own), 16 * (1 - own) + 16)])
    chord = (order[:, None] * 16 + np.arange(16)[None, :]).reshape(-1)
    f = lambda x: np.ascontiguousarray(x, dtype=np.float32)
    A = lambda x: f(x[order].reshape(16, 2, 64).transpose(1, 2, 0).reshape(128, 16))
    out = {}
    out["s_are"] = A(a_re)
    out["s_aim"] = A(a_im)
    out["s_ldt"] = f(np.repeat(log_dt[order].reshape(16, 2, 1), 64, axis=2).transpose(1, 2, 0).reshape(128, 16))
    Bf = lambda x: f(x[order].reshape(16, 2, 64, 16).transpose(1, 2, 0, 3).reshape(128, 16, 16))
    out["s_bre"] = Bf(b_re)
    out["s_bim"] = Bf(b_im)
    def Cf(x):
        xo = x[order].reshape(16, 2, 16, 64)
        r = np.zeros((2, 16, 16, 128), np.float32)
        r[0, :, :, 0:64] = xo[:, 0].transpose(1, 0, 2)
        r[1, :, :, 64:128] = xo[:, 1].transpose(1, 0, 2)
        return r
    out["s_cre"] = Cf(c_re)
    out["s_cim"] = Cf(c_im)
    out["s_d"] = f(d_skip[chord].reshape(4, 128).T)
    wg = w_glu[chord][:, chord[:256]]
    out["s_wglu"] = f(wg)
    E = np.zeros((2, 16, 4, 128), np.float32)
    for g2 in range(2):
        for o in range(16):
            for k in range(4):
                E[g2, o, k, k * 32 + g2 * 16 + o] = 1.0
    out["s_E"] = E
    out["s_tpos"] = f(np.broadcast_to(np.arange(LT + 1, dtype=np.float32), (128, LT + 1)))
    return out, chord


def ssm_setup(ctx, prm, consts, off):
    P = ctx.P
    Sx = {}
    o = off
    def T(name, shape, dt):
        nonlocal o
        t = ctx.tile(name, o, shape, dt)
        o = t.end
        return t
    cosT = T("cosT", [128, 16, LT], BF16)
    sinT = T("sinT", [128, 16, LT], BF16)
    Bre = T("Bbd_re", [128, 16, 128], BF16)
    Bim = T("Bbd_im", [128, 16, 128], BF16)
    Cre = T("Cbd_re", [128, 16, 128], BF16)
    Cimn = T("Cbd_imn", [128, 16, 128], BF16)
    rr = T("r", [128, 16], F32)
    cL = T("cL", [128, 16], F32)
    sL = T("sL", [128, 16], F32)
    Dt = T("Dsk", [128, 4], F32)
    Wg = T("Wglu", [128, 4, 256], BF16)
    persist_end = o
    are = T("are", [128, 16], F32); aim = T("aim", [128, 16], F32); dt_ = T("dt", [128, 16], F32)
    th = T("theta", [128, 16], F32)
    tpos = T("tpos", [128, LT + 1], F32)
    ph = T("phase", [128, LT], F32)
    braw = T("braw_re", [128, 16, 16], F32); biaw = T("braw_im", [128, 16, 16], F32)
    bbr = T("bbar_re", [128, 16, 16], F32); bbi = T("bbar_im", [128, 16, 16], F32)
    tmp = [T(f"st{i}", [128, 16], F32) for i in range(8)]
    tb = [T(f"stb{i}", [128, 16, 16], F32) for i in range(2)]
    Z = [T(f"Z{i}", [128, 128], BF16) for i in range(2)]
    Cl = {}
    for nm in ("cre", "cim"):
        for g2 in range(2):
            Cl[nm, g2] = T(f"Cl_{nm}{g2}", [16, 16, 128], BF16)
    Eb = [T(f"E{g2}", [16, 4, 128], BF16) for g2 in range(2)]

    ld = lambda t, src, q="sp": P.dma(q, lambda e: e.dma_start(out=t[:], in_=src), writes=[t.b])
    ld(are, prm["s_are"]); ld(aim, prm["s_aim"]); ld(dt_, prm["s_ldt"]); ld(tpos, prm["s_tpos"])
    ld(braw, prm["s_bre"]); ld(biaw, prm["s_bim"]); ld(Dt, prm["s_d"])
    for nm in ("cre", "cim"):
        for g2 in range(2):
            ld(Cl[nm, g2], prm["s_" + nm][g2], "pool")
    for g2 in range(2):
        ld(Eb[g2], prm["s_E"][g2], "pool")
    P.dma("pool", lambda e: e.dma_start(out=Wg[:], in_=prm["s_wglu"].rearrange("(c p) n -> p c n", p=128)), writes=[Wg.b])

    V = lambda fn, reads, writes: P.op("dve", fn, reads=[t.b for t in reads], writes=[t.b for t in writes])
    P.op("act", lambda e: e.activation(out=dt_[:], in_=dt_[:], func=AF.Exp), reads=[dt_.b], writes=[dt_.b])
    V(lambda e: e.tensor_tensor(out=th[:], in0=aim[:], in1=dt_[:], op=ALU.mult), [aim, dt_], [th])
    V(lambda e: e.tensor_tensor(out=rr[:], in0=are[:], in1=dt_[:], op=ALU.mult), [are, dt_], [rr])
    P.op("act", lambda e: e.activation(out=rr[:], in_=rr[:], func=AF.Exp), reads=[rr.b], writes=[rr.b])

    def sincos(out_sin, out_cos, gp, tcols, n):
        for dst, shift in ((out_sin, 0.0), (out_cos, np.pi / 2)):
            V(lambda e: e.tensor_scalar(out=ph[:, 0:n], in0=tpos[:, tcols], scalar1=th[:, gp:gp + 1], scalar2=None,
                                        op0=ALU.mult), [tpos, th], [ph])
            V(lambda e, shift=shift: e.tensor_scalar(out=ph[:, 0:n], in0=ph[:, 0:n], scalar1=float(shift + np.pi),
                                                     scalar2=TWO_PI, op0=ALU.add, op1=ALU.mod), [ph], [ph])
            P.op("act", lambda e, dst=dst: e.activation(out=dst, in_=ph[:, 0:n], func=AF.Sin, bias=negpi[:, 0:1]),
                 reads=[ph.b, negpi.b], writes=[dstb[0]])

    negpi = tmp[7]
    V(lambda e: e.memset(negpi[:], -float(np.pi)), [], [negpi])
    dstb = [None]
    s1, c1, kre, kim, den, t0, t1 = tmp[0], tmp[1], tmp[2], tmp[3], tmp[4], tmp[5], tmp[6]
    MAGIC = 12582912.0
    INV2PI = float(1.0 / TWO_PI)

    def range_sin(dst_ap, dst_buf, x, xk, n_shape):
        V(lambda e: e.tensor_scalar(out=xk, in0=x, scalar1=INV2PI, scalar2=MAGIC, op0=ALU.mult, op1=ALU.add), n_shape[0], n_shape[1])
        V(lambda e: e.tensor_scalar(out=xk, in0=xk, scalar1=-MAGIC, scalar2=None, op0=ALU.add), n_shape[1], n_shape[1])
        V(lambda e: e.scalar_tensor_tensor(out=x, in0=xk, scalar=-TWO_PI, in1=x, op0=ALU.mult, op1=ALU.add),
          n_shape[0] + n_shape[1], n_shape[0])
        P.op("act", lambda e: e.activation(out=dst_ap, in_=x, func=AF.Sin), reads=[t.b for t in n_shape[0]], writes=[dst_buf])

    for dst, shift in ((s1, 0.0), (c1, np.pi / 2)):
        V(lambda e, shift=shift: e.tensor_scalar(out=t0[:], in0=th[:], scalar1=float(shift), scalar2=None, op0=ALU.add), [th], [t0])
        range_sin(dst[:], dst.b, t0[:], t1[:], ([t0], [t1]))
    V(lambda e: e.tensor_tensor(out=c1[:], in0=c1[:], in1=rr[:], op=ALU.mult), [c1, rr], [c1])
    V(lambda e: e.tensor_scalar(out=c1[:], in0=c1[:], scalar1=-1.0, scalar2=None, op0=ALU.add), [c1], [c1])
    V(lambda e: e.tensor_tensor(out=s1[:], in0=s1[:], in1=rr[:], op=ALU.mult), [s1, rr], [s1])
    V(lambda e: e.tensor_tensor(out=den[:], in0=are[:], in1=are[:], op=ALU.mult), [are], [den])
    V(lambda e: e.tensor_tensor(out=t0[:], in0=aim[:], in1=aim[:], op=ALU.mult), [aim], [t0])
    V(lambda e: e.tensor_tensor(out=den[:], in0=den[:], in1=t0[:], op=ALU.add), [den, t0], [den])
    V(lambda e: e.reciprocal(out=den[:], in_=den[:]), [den], [den])
    V(lambda e: e.tensor_tensor(out=kre[:], in0=c1[:], in1=are[:], op=ALU.mult), [c1, are], [kre])
    V(lambda e: e.tensor_tensor(out=t0[:], in0=s1[:], in1=aim[:], op=ALU.mult), [s1, aim], [t0])
    V(lambda e: e.tensor_tensor(out=kre[:], in0=kre[:], in1=t0[:], op=ALU.add), [kre, t0], [kre])
    V(lambda e: e.tensor_tensor(out=kre[:], in0=kre[:], in1=den[:], op=ALU.mult), [kre, den], [kre])
    V(lambda e: e.tensor_tensor(out=kim[:], in0=s1[:], in1=are[:], op=ALU.mult), [s1, are], [kim])
    V(lambda e: e.tensor_tensor(out=t0[:], in0=c1[:], in1=aim[:], op=ALU.mult), [c1, aim], [t0])
    V(lambda e: e.tensor_tensor(out=kim[:], in0=kim[:], in1=t0[:], op=ALU.subtract), [kim, t0], [kim])
    V(lambda e: e.tensor_tensor(out=kim[:], in0=kim[:], in1=den[:], op=ALU.mult), [kim, den], [kim])
    kb = lambda k: k[:].unsqueeze(2).to_broadcast([128, 16, 16])
    V(lambda e: e.tensor_tensor(out=bbr[:], in0=braw[:], in1=kb(kre), op=ALU.mult), [braw, kre], [bbr])
    V(lambda e: e.tensor_tensor(out=tb[0][:], in0=biaw[:], in1=kb(kim), op=ALU.mult), [biaw, kim], [tb[0]])
    V(lambda e: e.tensor_tensor(out=bbr[:], in0=bbr[:], in1=tb[0][:], op=ALU.subtract), [bbr, tb[0]], [bbr])
    V(lambda e: e.tensor_tensor(out=bbi[:], in0=biaw[:], in1=kb(kre), op=ALU.mult), [biaw, kre], [bbi])
    V(lambda e: e.tensor_tensor(out=tb[1][:], in0=braw[:], in1=kb(kim), op=ALU.mult), [braw, kim], [tb[1]])
    V(lambda e: e.tensor_tensor(out=bbi[:], in0=bbi[:], in1=tb[1][:], op=ALU.add), [bbi, tb[1]], [bbi])
    n = 0
    for src, dstT in ((bbr, Bre), (bbi, Bim)):
        for gp in range(16):
            k = gp % 4
            z = Z[n % 2]
            pb = 6 + n % 2
            n += 1
            V(lambda e, z=z: e.memset(z[:], 0.0), [], [z])
            V(lambda e, z=z, src=src, gp=gp, k=k: e.tensor_copy(out=z[0:64, k * 32:k * 32 + 16], in_=src[0:64, gp, :]),
              [src], [z])
            V(lambda e, z=z, src=src, gp=gp, k=k: e.tensor_copy(out=z[64:128, k * 32 + 16:k * 32 + 32], in_=src[64:128, gp, :]),
              [src], [z])
            P.op("pe", lambda e, z=z, pb=pb: e.transpose(ctx.psbf(pb)[:, 0:128], z[:], consts["ident"]),
                 reads=[z.b, consts["buf"]], writes=[ctx.psb[pb]])
            V2 = P.op("act", lambda e, dstT=dstT, gp=gp, pb=pb: e.activation(out=dstT[:, gp, :], in_=ctx.psbf(pb)[:, 0:128],
                                                                           func=AF.Identity),
                      reads=[ctx.psb[pb]], writes=[dstT.b])
    for nm, dstT, sc in (("cre", Cre, 1.0), ("cim", Cimn, -1.0)):
        for gp in range(16):
            k = gp % 4
            pb = 6 + n % 2
            n += 1
            for g2 in range(2):
                P.op("pe", lambda e, nm=nm, g2=g2, gp=gp, k=k, pb=pb: e.matmul(
                    ctx.ps[pb][:, 0:128], lhsT=Cl[nm, g2][:, gp, :], rhs=Eb[g2][:, k, :], start=(g2 == 0), stop=(g2 == 1)),
                    reads=[Cl[nm, g2].b, Eb[g2].b], writes=[ctx.psb[pb]])
            P.op("act", lambda e, dstT=dstT, gp=gp, pb=pb, sc=sc: e.activation(out=dstT[:, gp, :], in_=ctx.ps[pb][:, 0:128],
                                                                              func=AF.Identity, scale=sc),
                 reads=[ctx.psb[pb]], writes=[dstT.b])
    ph2 = T("phase2", [128, LT], F32)
    for gp in range(16):
        for dst, shift in ((sinT, 0.0), (cosT, np.pi / 2)):
            V(lambda e, gp=gp, shift=shift: e.tensor_scalar(out=ph[:, 0:LT], in0=tpos[:, 0:LT], scalar1=th[:, gp:gp + 1],
                                                          scalar2=float(shift), op0=ALU.mult, op1=ALU.add), [tpos, th], [ph])
            range_sin(dst[:, gp, :], dst.b, ph[:, 0:LT], ph2[:, 0:LT], ([ph], [ph2]))
    for dst, shift in ((sL, 0.0), (cL, np.pi / 2)):
        V(lambda e, shift=shift: e.tensor_scalar(out=t0[:], in0=th[:], scalar1=float(LT), scalar2=float(shift), op0=ALU.mult,
                                                 op1=ALU.add), [th], [t0])
        range_sin(dst[:], dst.b, t0[:], t1[:], ([t0], [t1]))
    Sx.update(cosT=cosT, sinT=sinT, Bre=Bre, Bim=Bim, Cre=Cre, Cimn=Cimn, r=rr, cL=cL, sL=sL, D=Dt, Wg=Wg, end=persist_end)
    return Sx


def ssm_pass(ctx, Sx, consts, w_d, col0, hT_d, hbuf, og_d, row0, ogbuf, off, ntiles=NT):
    P = ctx.P
    o = off
    def T(name, shape, dt):
        nonlocal o
        t = ctx.tile(name, o, shape, dt)
        o = t.end
        return t
    W = T("Wssm", [128, 16, 768], BF16)
    ht = [T(f"sht{i}", [128, 16, 512], BF16) for i in range(2)]
    uT = [T(f"uT{i}", [128, 4, 512], BF16) for i in range(2)]
    sgs = [T(f"sgs{i}", [128, 2, 512], BF16) for i in range(2)]
    st = []
    for k in range(2):
        st.append({n: T(f"{n}{k}", [128, 512], F32) for n in ("t1", "t2", "Xre", "Xim", "Gre", "Gim", "p1", "p2")})
        st[k]["hre"] = T(f"hre{k}", [128, 512], BF16)
        st[k]["him"] = T(f"him{k}", [128, 512], BF16)
    ini_re = T("ini_re", [128, 16], F32)
    ini_im = T("ini_im", [128, 16], F32)
    tc = T("tcarry", [128, 2], F32)
    yd = T("yd", [128, 4, 512], F32)
    tg = T("tg", [128, 4, 512], F32)
    ygb = T("ygb", [128, 4, 512], BF16)
    sgl = [T(f"sgl{i}", [128, 512], F32) for i in range(2)]
    ob = [T(f"ob{i}", [128, 512], BF16) for i in range(2)]
    print("ssm sbuf end", o)
    cosT, sinT, r, cL, sL = Sx["cosT"], Sx["sinT"], Sx["r"], Sx["cL"], Sx["sL"]
    V = lambda fn, reads, writes: P.op("dve", fn, reads=reads, writes=writes)
    G = lambda fn, reads, writes: P.op("pool", fn, reads=reads, writes=writes)
    P.dma("pool", lambda e: e.dma_start(out=W[:], in_=w_d[:, col0:col0 + 768].rearrange("(k p) f -> p k f", p=128)),
          writes=[W.b])
    V(lambda e: e.memset(ini_re[:], 0.0), [], [ini_re.b])
    V(lambda e: e.memset(ini_im[:], 0.0), [], [ini_im.b])
    nb = 0
    for tt in range(ntiles):
        h_ = ht[tt % 2]
        u_ = uT[tt % 2]
        g_ = sgs[tt % 2]
        ts = slice(tt * 512, (tt + 1) * 512)
        P.dma("sp", lambda e, h_=h_, tt=tt: e.dma_start(out=h_[:], in_=hT_d[:, :, tt * 512:(tt + 1) * 512]),
              reads=[hbuf], writes=[h_.b])
        for fc in range(6):
            pb = nb % 2
            nb += 1
            for kc in range(16):
                P.op("pe", lambda e, pb=pb, kc=kc, fc=fc, h_=h_: e.matmul(
                    ctx.ps[pb][:, :], lhsT=W[:, kc, fc * 128:(fc + 1) * 128], rhs=h_[:, kc, :],
                    start=(kc == 0), stop=(kc == 15)), reads=[W.b, h_.b], writes=[ctx.psb[pb]])
            if fc < 4:
                P.op("act", lambda e, pb=pb, u_=u_, fc=fc: e.activation(out=u_[:, fc, :], in_=ctx.ps[pb][:, :], func=AF.Identity),
                     reads=[ctx.psb[pb]], writes=[u_.b])
            else:
                P.op("act", lambda e, pb=pb, g_=g_, fc=fc: e.activation(out=g_[:, fc - 4, :], in_=ctx.ps[pb][:, :], func=AF.Silu),
                     reads=[ctx.psb[pb]], writes=[g_.b])
        for gp in range(16):
            c = gp // 4
            k = gp % 2
            S_ = st[k]
            ba, bb = (2, 3) if k == 0 else (4, 5)
            P.op("pe", lambda e, gp=gp, c=c, ba=ba, u_=u_: e.matmul(ctx.ps[ba][:, :], lhsT=Sx["Bre"][:, gp, :], rhs=u_[:, c, :],
                                                                start=True, stop=True),
                 reads=[Sx["Bre"].b, u_.b], writes=[ctx.psb[ba]])
            P.op("pe", lambda e, gp=gp, c=c, bb=bb, u_=u_: e.matmul(ctx.ps[bb][:, :], lhsT=Sx["Bim"][:, gp, :], rhs=u_[:, c, :],
                                                                start=True, stop=True),
                 reads=[Sx["Bim"].b, u_.b], writes=[ctx.psb[bb]])
            co, si = cosT[:, gp, :], sinT[:, gp, :]
            V(lambda e, S_=S_, ba=ba, co=co: e.tensor_tensor(out=S_["t1"][:], in0=ctx.ps[ba][:, :], in1=co, op=ALU.mult),
              [ctx.psb[ba], cosT.b], [S_["t1"].b])
            V(lambda e, S_=S_, ba=ba, si=si: e.tensor_tensor(out=S_["Xim"][:], in0=ctx.ps[ba][:, :], in1=si, op=ALU.mult),
              [ctx.psb[ba], sinT.b], [S_["Xim"].b])
            V(lambda e, S_=S_, bb=bb, si=si: e.tensor_tensor(out=S_["t2"][:], in0=ctx.ps[bb][:, :], in1=si, op=ALU.mult),
              [ctx.psb[bb], sinT.b], [S_["t2"].b])
            V(lambda e, S_=S_, bb=bb, co=co: e.tensor_tensor(out=S_["Xre"][:], in0=ctx.ps[bb][:, :], in1=co, op=ALU.mult),
              [ctx.psb[bb], cosT.b], [S_["Xre"].b])
            G(lambda e, S_=S_: e.tensor_tensor(out=S_["Xim"][:], in0=S_["Xre"][:], in1=S_["Xim"][:], op=ALU.subtract),
              [S_["Xre"].b, S_["Xim"].b], [S_["Xim"].b])
            G(lambda e, S_=S_: e.tensor_tensor(out=S_["Xre"][:], in0=S_["t1"][:], in1=S_["t2"][:], op=ALU.add),
              [S_["t1"].b, S_["t2"].b], [S_["Xre"].b])
            rb = r[:, gp:gp + 1].to_broadcast([128, 512])
            V(lambda e, S_=S_, rb=rb, gp=gp: e.tensor_tensor_scan(out=S_["Gre"][:], data0=rb, data1=S_["Xre"][:],
                                                               initial=ini_re[:, gp:gp + 1], op0=ALU.mult, op1=ALU.add),
              [r.b, S_["Xre"].b, ini_re.b], [S_["Gre"].b])
            V(lambda e, S_=S_, rb=rb, gp=gp: e.tensor_tensor_scan(out=S_["Gim"][:], data0=rb, data1=S_["Xim"][:],
                                                               initial=ini_im[:, gp:gp + 1], op0=ALU.mult, op1=ALU.add),
              [r.b, S_["Xim"].b, ini_im.b], [S_["Gim"].b])
            gl_re, gl_im = S_["Gre"][:, 511:512], S_["Gim"][:, 511:512]
            V(lambda e, gp=gp, gl_im=gl_im: e.tensor_tensor(out=tc[:, 0:1], in0=gl_im, in1=sL[:, gp:gp + 1], op=ALU.mult),
              [S_["Gim"].b, sL.b], [tc.b])
            V(lambda e, gp=gp, gl_im=gl_im: e.tensor_tensor(out=tc[:, 1:2], in0=gl_im, in1=cL[:, gp:gp + 1], op=ALU.mult),
              [S_["Gim"].b, cL.b], [tc.b])
            V(lambda e, gp=gp, gl_re=gl_re: e.scalar_tensor_tensor(out=ini_re[:, gp:gp + 1], in0=gl_re, scalar=cL[:, gp:gp + 1],
                                                                  in1=tc[:, 0:1], op0=ALU.mult, op1=ALU.subtract),
              [S_["Gre"].b, cL.b, tc.b], [ini_re.b])
            V(lambda e, gp=gp, gl_re=gl_re: e.scalar_tensor_tensor(out=ini_im[:, gp:gp + 1], in0=gl_re, scalar=sL[:, gp:gp + 1],
                                                                  in1=tc[:, 1:2], op0=ALU.mult, op1=ALU.add),
              [S_["Gre"].b, sL.b, tc.b], [ini_im.b])
            G(lambda e, S_=S_, co=co: e.tensor_tensor(out=S_["p1"][:], in0=S_["Gre"][:], in1=co, op=ALU.mult),
              [S_["Gre"].b, cosT.b], [S_["p1"].b])
            G(lambda e, S_=S_, si=si: e.tensor_tensor(out=S_["p2"][:], in0=S_["Gim"][:], in1=si, op=ALU.mult),
              [S_["Gim"].b, sinT.b], [S_["p2"].b])
            G(lambda e, S_=S_: e.tensor_tensor(out=S_["hre"][:], in0=S_["p1"][:], in1=S_["p2"][:], op=ALU.subtract),
              [S_["p1"].b, S_["p2"].b], [S_["hre"].b])
            G(lambda e, S_=S_, si=si: e.tensor_tensor(out=S_["p1"][:], in0=S_["Gre"][:], in1=si, op=ALU.mult),
              [S_["Gre"].b, sinT.b], [S_["p1"].b])
            G(lambda e, S_=S_, co=co: e.tensor_tensor(out=S_["p2"][:], in0=S_["Gim"][:], in1=co, op=ALU.mult),
              [S_["Gim"].b, cosT.b], [S_["p2"].b])
            G(lambda e, S_=S_: e.tensor_tensor(out=S_["him"][:], in0=S_["p1"][:], in1=S_["p2"][:], op=ALU.add),
              [S_["p1"].b, S_["p2"].b], [S_["him"].b])
            P.op("pe", lambda e, gp=gp, S_=S_: e.matmul(ctx.ps[6][:, :], lhsT=Sx["Cre"][:, gp, :], rhs=S_["hre"][:],
                                                    start=(gp % 4 == 0), stop=False),
                 reads=[Sx["Cre"].b, S_["hre"].b], writes=[ctx.psb[6]])
            P.op("pe", lambda e, gp=gp, S_=S_: e.matmul(ctx.ps[6][:, :], lhsT=Sx["Cimn"][:, gp, :], rhs=S_["him"][:],
                                                    start=False, stop=(gp % 4 == 3)),
                 reads=[Sx["Cimn"].b, S_["him"].b], writes=[ctx.psb[6]])
            if gp % 4 == 3:
                V(lambda e, c=c, u_=u_: e.scalar_tensor_tensor(out=yd[:, c, :], in0=u_[:, c, :], scalar=Sx["D"][:, c:c + 1],
                                                            in1=ctx.ps[6][:, :], op0=ALU.mult, op1=ALU.add),
                  [u_.b, Sx["D"].b, ctx.psb[6]], [yd.b])
        ydf = yd[:].rearrange("p c t -> p (c t)")
        tgf = tg[:].rearrange("p c t -> p (c t)")
        ygf = ygb[:].rearrange("p c t -> p (c t)")
        G(lambda e: e.tensor_tensor(out=tgf, in0=ydf, in1=ydf, op=ALU.mult), [yd.b], [tg.b])
        G(lambda e: e.tensor_scalar(out=tgf, in0=tgf, scalar1=0.044715, scalar2=1.0, op0=ALU.mult, op1=ALU.add), [tg.b], [tg.b])
        G(lambda e: e.tensor_tensor(out=tgf, in0=tgf, in1=ydf, op=ALU.mult), [tg.b, yd.b], [tg.b])
        P.op("act", lambda e: e.activation(out=tgf, in_=tgf, func=AF.Sigmoid, scale=1.5957691216057308),
             reads=[tg.b], writes=[tg.b])
        V(lambda e: e.tensor_tensor(out=ygf, in0=tgf, in1=ydf, op=ALU.mult), [tg.b, yd.b], [ygb.b])
        for oc in range(2):
            for ci in range(4):
                P.op("pe", lambda e, oc=oc, ci=ci: e.matmul(ctx.ps[7][:, :], lhsT=Sx["Wg"][:, ci, oc * 128:(oc + 1) * 128],
                                                         rhs=ygb[:, ci, :], start=(ci == 0), stop=(ci == 3)),
                     reads=[Sx["Wg"].b, ygb.b], writes=[ctx.psb[7]])
            sg_ = sgl[oc]
            o_ = ob[oc]
            P.op("act", lambda e, sg_=sg_: e.activation(out=sg_[:], in_=ctx.ps[7][:, :], func=AF.Sigmoid),
                 reads=[ctx.psb[7]], writes=[sg_.b])
            V(lambda e, sg_=sg_, oc=oc: e.tensor_tensor(out=sg_[:], in0=sg_[:], in1=ygb[:, oc, :], op=ALU.mult),
              [sg_.b, ygb.b], [sg_.b])
            V(lambda e, sg_=sg_, o_=o_, oc=oc, g_=g_: e.tensor_tensor(out=o_[:], in0=sg_[:], in1=g_[:, oc, :], op=ALU.mult),
              [sg_.b, g_.b], [o_.b])
            P.dma("sp", lambda e, o_=o_, oc=oc, ts=ts: e.dma_start(out=og_d[row0 + oc * 128:row0 + (oc + 1) * 128, ts], in_=o_[:]),
                  reads=[o_.b], writes=[ogbuf])


DN_ALPHA = 4.0 ** 0.25
LN_EPS = 1e-5


def phase_outproj_ln(ctx, og_d, ogbuf, w_out_d, x_d, xbuf, gate_bc, lng_d, lnb_d, out_d, outbuf, consts, off,
                     ntok=4096, adaT_next=None, h1T_d=None, h1buf=None, tok0_h1=0):
    P = ctx.P
    o = off
    def T(name, shape, dt):
        nonlocal o
        t = ctx.tile(name, o, shape, dt)
        o = t.end
        return t
    Wo = T("Wo", [128, 16, 2048], BF16)
    lng = T("lng", [128, 2048], F32)
    lnb = T("lnb", [128, 2048], F32)
    ogt = [T(f"ogt{i}", [128, 16, 512], BF16) for i in range(2)]
    xt = [T(f"xres{i}", [128, 2048], F32) for i in range(2)]
    z = T("zres", [128, 2048], F32)
    xo = [T(f"xo{i}", [128, 2048], F32) for i in range(2)]
    stats = T("bnst", [128, 4, 6], F32)
    mv = T("bnmv", [128, 2], F32)
    rstd = T("rstd", [128, 1], F32)
    if adaT_next is not None:
        xb = T("xob", [128, 2048], BF16)
        h1 = T("h1t", [128, 16, 512], BF16)
    print("phaseB sbuf end", o)
    for half in range(2):
        P.dma("pool", lambda e, half=half: e.dma_start(
            out=Wo[:, half * 8:(half + 1) * 8, :],
            in_=w_out_d[half * 1024:(half + 1) * 1024, :].rearrange("(k p) f -> p k f", p=128)), writes=[Wo.b])
    P.dma("sp", lambda e: e.dma_start(out=lng[:], in_=lng_d.partition_broadcast(128)), writes=[lng.b])
    P.dma("sp", lambda e: e.dma_start(out=lnb[:], in_=lnb_d.partition_broadcast(128)), writes=[lnb.b])
    V = lambda fn, reads, writes: P.op("dve", fn, reads=reads, writes=writes)
    G = lambda fn, reads, writes: P.op("pool", fn, reads=reads, writes=writes)
    nblk = ntok // 128
    for blk in range(nblk):
        tt, bi = divmod(blk, 4)
        og_ = ogt[tt % 2]
        if bi == 0:
            P.dma("sp", lambda e, og_=og_, tt=tt: e.dma_start(
                out=og_[:], in_=og_d[:, tt * 512:(tt + 1) * 512].rearrange("(k p) t -> p k t", p=128)),
                reads=[ogbuf], writes=[og_.b])
        x_ = xt[blk % 2]
        xo_ = xo[blk % 2]
        P.dma("act", lambda e, x_=x_, blk=blk: e.dma_start(out=x_[:], in_=x_d[blk * 128:(blk + 1) * 128, :]),
              reads=[xbuf], writes=[x_.b])
        for nc_ in range(4):
            for fc in range(16):
                P.op("pe", lambda e, nc_=nc_, fc=fc, og_=og_, bi=bi: e.matmul(
                    ctx.ps[nc_][:, :], lhsT=og_[:, fc, bi * 128:(bi + 1) * 128], rhs=Wo[:, fc, nc_ * 512:(nc_ + 1) * 512],
                    start=(fc == 0), stop=(fc == 15)), reads=[og_.b, Wo.b], writes=[ctx.psb[nc_]])
            cs = slice(nc_ * 512, (nc_ + 1) * 512)
            V(lambda e, nc_=nc_, cs=cs: e.tensor_tensor(out=z[:, cs], in0=ctx.ps[nc_][:, :], in1=gate_bc[:, cs], op=ALU.mult),
              [ctx.psb[nc_], gate_bc.b], [z.b])
        V(lambda e, x_=x_: e.scalar_tensor_tensor(out=z[:], in0=x_[:], scalar=DN_ALPHA, in1=z[:], op0=ALU.mult, op1=ALU.add),
          [x_.b, z.b], [z.b])
        for c4 in range(4):
            V(lambda e, c4=c4: e.bn_stats(out=stats[:, c4, :], in_=z[:, c4 * 512:(c4 + 1) * 512]), [z.b], [stats.b])
        V(lambda e: e.bn_aggr(out=mv[:], in_=stats[:].rearrange("p c s -> p (c s)")), [stats.b], [mv.b])
        V(lambda e: e.tensor_scalar(out=rstd[:], in0=mv[:, 1:2], scalar1=LN_EPS, scalar2=None, op0=ALU.add), [mv.b], [rstd.b])
        P.op("act", lambda e: e.activation(out=rstd[:], in_=rstd[:], func=AF.Sqrt), reads=[rstd.b], writes=[rstd.b])
        V(lambda e: e.reciprocal(out=rstd[:], in_=rstd[:]), [rstd.b], [rstd.b])
        V(lambda e: e.tensor_scalar(out=z[:], in0=z[:], scalar1=mv[:, 0:1], scalar2=rstd[:, 0:1], op0=ALU.subtract, op1=ALU.mult),
          [z.b, mv.b, rstd.b], [z.b])
        G(lambda e, xo_=xo_: e.tensor_tensor(out=xo_[:], in0=z[:], in1=lng[:], op=ALU.mult), [z.b, lng.b], [xo_.b])
        G(lambda e, xo_=xo_: e.tensor_tensor(out=xo_[:], in0=xo_[:], in1=lnb[:], op=ALU.add), [xo_.b, lnb.b], [xo_.b])
        P.dma("sp", lambda e, xo_=xo_, blk=blk: e.dma_start(out=out_d[blk * 128:(blk + 1) * 128, :], in_=xo_[:]),
              reads=[xo_.b], writes=[outbuf])
        if adaT_next is not None:
            P.op("act", lambda e, xo_=xo_: e.activation(out=xb[:], in_=xo_[:], func=AF.Identity), reads=[xo_.b], writes=[xb.b])
            for kc in range(16):
                pb = 4 + kc % 4
                P.op("pe", lambda e, kc=kc, pb=pb: e.transpose(ctx.psbf(pb)[:, 0:128], xb[:, kc * 128:(kc + 1) * 128],
                                                             consts["ident"]),
                     reads=[xb.b, consts["buf"]], writes=[ctx.psb[pb]])
                P.op("act", lambda e, kc=kc, pb=pb, bi=bi: e.activation(
                    out=h1[:, kc, bi * 128:(bi + 1) * 128], in_=ctx.psbf(pb)[:, 0:128], func=AF.Identity,
                    scale=adaT_next[:, 16 + kc:17 + kc], bias=adaT_next[:, kc:kc + 1]),
                    reads=[ctx.psb[pb], adaT_next.b], writes=[h1.b])
            if bi == 3:
                P.dma("sp", lambda e, tt=tt: e.dma_start(
                    out=h1T_d[:, :, tok0_h1 + tt * 512:tok0_h1 + (tt + 1) * 512], in_=h1[:]), reads=[h1.b], writes=[h1buf])


NEG30 = -1.0e30


def make_moba_consts_np():
    esel = np.zeros((32, 32, 128), np.float32)
    for n in range(32):
        esel[n, n, :] = 1.0
    m = np.zeros((128, 128), np.float32)
    m[:, 32:64] = NEG30
    m[:, 64 + 32:128] = -BIG
    return esel.reshape(32, 32 * 128).astype(ml_dtypes.bfloat16), m


def moba_alloc(ctx, esel_d, m64_d, off):
    P = ctx.P
    o = off
    def T(name, shape, dt):
        nonlocal o
        t = ctx.tile(name, o, shape, dt)
        o = t.end
        return t
    M = {}
    M["esel"] = T("esel", [32, 32 * 128], BF16)
    M["m64"] = T("m64", [128, 128], F32)
    M["MbT"] = T("MbT", [32, S], BF16)
    M["km"] = T("km", [128, 32], F32)
    M["kmh"] = T("kmh", [128, 32], BF16)
    M["kml"] = T("kml", [128, 32], BF16)
    M["gm"] = T("gm", [128, 16, 32], F32)
    M["top8"] = T("top8", [128, 16, 8], F32)
    M["biasb"] = T("biasb", [128, 16, 32], BF16)
    M["pT"] = [T(f"pT{i}", [128, 512], BF16) for i in range(2)]
    M["psum"] = [T(f"psacc{i}", [128, 512], F32) for i in range(2)]
    M["rec"] = [T(f"rec{i}", [128, 512], F32) for i in range(2)]
    M["og"] = [T(f"mog{i}", [128, 512], BF16) for i in range(2)]
    M["onesf"] = T("onesf", [128, 128], F32)
    M["end"] = o
    P.dma("sp", lambda e: e.dma_start(out=M["esel"][:], in_=esel_d), writes=[M["esel"].b])
    P.dma("sp", lambda e: e.dma_start(out=M["m64"][:], in_=m64_d), writes=[M["m64"].b])
    P.op("dve", lambda e: e.memset(M["onesf"][:], 1.0), writes=[M["onesf"].b])
    return M


def moba_head(ctx, T, M, consts, og_d, row0, ogbuf):
    P = ctx.P
    scale = 128 ** -0.5
    qT, kT, v, sgT = T["qT"], T["kT"], T["v"], T["sgT"]
    cb = consts["buf"]
    V = lambda fn, reads, writes: P.op("dve", fn, reads=reads, writes=writes)
    km, kmh, kml, gm, top8, biasb, MbT, m64 = M["km"], M["kmh"], M["kml"], M["gm"], M["top8"], M["biasb"], M["MbT"], M["m64"]
    V(lambda e: e.tensor_reduce(out=km[:], in_=kT[:, :].rearrange("p (n k) -> p n k", k=256), axis=AX.X, op=ALU.add),
      [kT.b], [km.b])
    V(lambda e: e.tensor_scalar(out=km[:], in0=km[:], scalar1=1.0 / 256.0, scalar2=None, op0=ALU.mult), [km.b], [km.b])
    V(lambda e: e.tensor_copy(out=kmh[:], in_=km[:]), [km.b], [kmh.b])
    V(lambda e: e.tensor_tensor(out=km[:], in0=km[:], in1=kmh[:], op=ALU.subtract), [km.b, kmh.b], [km.b])
    V(lambda e: e.tensor_copy(out=kml[:], in_=km[:]), [km.b], [kml.b])
    for q16 in range(4):
        for j in range(16):
            qb = q16 * 16 + j
            qs = slice(qb * 128, (qb + 1) * 128)
            P.op("pe", lambda e, j=j, qs=qs: e.matmul(ctx.ps[4][:, j * 32:(j + 1) * 32], lhsT=qT[:, qs], rhs=kmh[:],
                                                      start=True, stop=False), reads=[qT.b, kmh.b], writes=[ctx.psb[4]])
            P.op("pe", lambda e, j=j, qs=qs: e.matmul(ctx.ps[4][:, j * 32:(j + 1) * 32], lhsT=qT[:, qs], rhs=kml[:],
                                                      start=False, stop=True), reads=[qT.b, kml.b], writes=[ctx.psb[4]])
        for j in range(16):
            qb = q16 * 16 + j
            nb = qb // 2
            V(lambda e, j=j, nb=nb: e.tensor_tensor(out=gm[:, j, :], in0=ctx.ps[4][:, j * 32:(j + 1) * 32],
                                                    in1=m64[:, 32 - nb:64 - nb], op=ALU.add),
              [ctx.psb[4], m64.b], [gm.b])
            V(lambda e, j=j: e.max(out=top8[:, j, :], in_=gm[:, j, :]), [gm.b], [top8.b])
            V(lambda e, j=j: e.tensor_scalar(out=gm[:, j, :], in0=gm[:, j, :], scalar1=top8[:, j, 2:3], scalar2=-BIG,
                                             op0=ALU.is_lt, op1=ALU.mult), [gm.b, top8.b], [gm.b])
            V(lambda e, j=j, nb=nb: e.tensor_tensor(out=biasb[:, j, :], in0=gm[:, j, :], in1=m64[:, 64 + 32 - nb:128 - nb],
                                                    op=ALU.add), [gm.b, m64.b], [biasb.b])
        for j in range(16):
            tbk = 5 if j < 8 else 7
            P.op("pe", lambda e, j=j, tbk=tbk: e.transpose(ctx.psbf(tbk)[0:32, (j % 8) * 128:(j % 8 + 1) * 128], biasb[:, j, :],
                                                         consts["ident"]),
                 reads=[biasb.b, cb], writes=[ctx.psb[tbk]])
        for hh in range(2):
            tbk = 5 if hh == 0 else 7
            P.op("act", lambda e, q16=q16, hh=hh, tbk=tbk: e.activation(
                out=MbT[:, q16 * 2048 + hh * 1024:q16 * 2048 + (hh + 1) * 1024], in_=ctx.psbf(tbk)[0:32, 0:1024],
                func=AF.Identity), reads=[ctx.psb[tbk]], writes=[MbT.b])
    steps = []
    for G in range(int(os.environ.get("NG", NT))):
        for kb in range(0, 4 * G + 4):
            steps.append((G, kb))
    n = len(steps)

    def geom(i):
        G, kb = steps[i]
        r = kb - 4 * G
        c0 = 0 if r <= 0 else r * 128
        s0 = 0 if r < 0 else (256 if r in (0, 1) else None)
        return G, kb, r, c0, s0, i % 2, 2 + G % 2

    def Zmm(i):
        G, kb, r, c0, s0, zb, Ob = geom(i)
        ks = slice(kb * 128, (kb + 1) * 128)
        q0 = G * 512
        z = ctx.ps[zb]
        if kb == 0:
            P.op("pe", lambda e: e.matmul(ctx.ps[Ob][:, 0:512], lhsT=consts["zeros"], rhs=qT[:, q0:q0 + 512],
                                          start=True, stop=False, skip_group_check=True), reads=[cb, qT.b], writes=[ctx.psb[Ob]])
        P.op("pe", lambda e: e.matmul(z[:, c0:512], lhsT=kT[:, ks], rhs=qT[:, q0 + c0:q0 + 512], start=True,
                                      stop=(s0 is None and r < 0)), reads=[kT.b, qT.b], writes=[ctx.psb[zb]])
        if s0 is not None:
            nblk = kb // 2
            P.op("pe", lambda e: e.matmul(z[:, s0:512], lhsT=M["esel"][:, nblk * 128:(nblk + 1) * 128],
                                          rhs=MbT[:, q0 + s0:q0 + 512], start=False, stop=(r < 0)),
                 reads=[M["esel"].b, MbT.b], writes=[ctx.psb[zb]])
        if r >= 0:
            P.op("pe", lambda e: e.matmul(z[:, c0:c0 + 128], lhsT=consts["ident"], rhs=consts["mbmask"], start=False, stop=True),
                 reads=[cb], writes=[ctx.psb[zb]])

    def Pact(i):
        G, kb, r, c0, s0, zb, Ob = geom(i)
        p_ = M["pT"][zb]
        P.op("act", lambda e: e.activation(out=p_[:, c0:512], in_=ctx.ps[zb][:, c0:512], func=AF.Exp, scale=scale),
             reads=[ctx.psb[zb]], writes=[p_.b])

    def PV(i):
        G, kb, r, c0, s0, zb, Ob = geom(i)
        p_ = M["pT"][zb]
        acc = M["psum"][G % 2]
        ks = slice(kb * 128, (kb + 1) * 128)
        P.op("pe", lambda e: e.matmul(ctx.ps[Ob][:, c0:512], lhsT=v[:, ks], rhs=p_[:, c0:512], start=False, stop=False,
                                      skip_group_check=True), reads=[v.b, p_.b], writes=[ctx.psb[Ob]])
        if kb == 0:
            V(lambda e: e.tensor_copy(out=acc[:], in_=p_[:]), [p_.b], [acc.b])
        else:
            V(lambda e: e.tensor_tensor(out=acc[:, c0:512], in0=acc[:, c0:512], in1=p_[:, c0:512], op=ALU.add),
              [acc.b, p_.b], [acc.b])
        if kb == 4 * G + 3:
            rec_, og_ = M["rec"][G % 2], M["og"][G % 2]
            P.op("pe", lambda e: e.matmul(ctx.ps[6][:, :], lhsT=M["onesf"][:], rhs=acc[:], start=True, stop=True),
                 reads=[M["onesf"].b, acc.b], writes=[ctx.psb[6]])
            V(lambda e: e.reciprocal(out=rec_[:], in_=ctx.ps[6][:, :]), [ctx.psb[6]], [rec_.b])
            V(lambda e: e.tensor_tensor(out=rec_[:], in0=ctx.ps[Ob][:, :], in1=rec_[:], op=ALU.mult), [ctx.psb[Ob], rec_.b], [rec_.b])
            V(lambda e: e.tensor_tensor(out=og_[:], in0=rec_[:], in1=sgT[:, G * 512:(G + 1) * 512], op=ALU.mult),
              [rec_.b, sgT.b], [og_.b])
            P.dma("sp", lambda e: e.dma_start(out=og_d[row0:row0 + 128, G * 512:(G + 1) * 512], in_=og_[:]),
                  reads=[og_.b], writes=[ogbuf])

    Zmm(0)
    for i in range(n):
        if i + 1 < n:
            Zmm(i + 1)
        Pact(i)
        PV(i)


SSM_SHAPES = {"s_are": [128, 16], "s_aim": [128, 16], "s_ldt": [128, 16], "s_bre": [128, 16, 16], "s_bim": [128, 16, 16],
              "s_cre": [2, 16, 16, 128], "s_cim": [2, 16, 16, 128], "s_d": [128, 4], "s_wglu": [512, 256],
              "s_E": [2, 16, 4, 128], "s_tpos": [128, LT + 1]}
NH0 = 6
NH1 = 8


def build_prog_A():
    nc = bass.Bass("TRN2", target_bir_lowering=False)
    I = lambda n, shp, dt=F32: nc.dram_tensor(n, shp, dt, kind="ExternalInput").ap()
    x_b = I("x_b", [S, D]); c16 = I("c16", [16, 128]); ada_w = I("ada_w", [D, 3 * D]); ada_b = I("ada_b", [48, 128])
    cst = I("consts", [128, NCONST], BF16); w0 = I("w0", [D, NH0 * 512 + 768])
    prm = {k: I(k, v) for k, v in SSM_SHAPES.items()}
    og_d = nc.dram_tensor("og_out", [NH0 * 128 + 256, S], BF16, kind="ExternalOutput").ap()
    hT_d = nc.dram_tensor("hT_scr", [128, 16, S], BF16).ap()
    scr_gate = nc.dram_tensor("scr_gate", [16, 128], F32).ap()
    ctx = Ctx(nc)
    P = ctx.P
    consts = load_consts(ctx, cst, 0)
    adaT = ctx.tile("adaT", consts["end"], [128, 48], F32)
    gate_bc = ctx.tile("gate_bc", adaT.end, [128, D], F32)
    base = gate_bc.end
    phase_ada(ctx, c16, ada_w, ada_b, base, adaT, scr_gate, gate_bc)
    P.barrier()
    hbuf = phase_prepass(ctx, x_b, hT_d, adaT, consts, base)
    P.barrier()
    T = alloc_head_tiles(ctx, base)
    A = alloc_attn_tmp(ctx, T["end"])
    ogbuf = P.buf("og_out")
    for h in range(NH0):
        inproj_head(ctx, T, w0, h * 512, hT_d, hbuf)
        sb_attention_head(ctx, T, A, consts, og_d, h * 128, ogbuf)
    P.barrier()
    Sx = ssm_setup(ctx, prm, consts, base)
    P.barrier()
    ssm_pass(ctx, Sx, consts, w0, NH0 * 512, hT_d, hbuf, og_d, NH0 * 128, ogbuf, Sx["end"])
    P.emit()
    return nc


def build_prog_BD(with_next):
    nc = bass.Bass("TRN2", target_bir_lowering=False)
    I = lambda n, shp, dt=F32: nc.dram_tensor(n, shp, dt, kind="ExternalInput").ap()
    NTOK = S // 2
    cst = I("consts", [128, NCONST], BF16)
    og_in = I("og_in", [D, NTOK], BF16); w_out = I("w_out", [D, D]); x_d = I("x_d", [NTOK, D]); c16 = I("c16", [16, 128])
    nl = 2 if with_next else 1
    ada_w = I("ada_w", [nl, D, 3 * D]); ada_b = I("ada_b", [nl, 48, 128])
    lng = I("lng", [D]); lnb = I("lnb", [D])
    out_d = nc.dram_tensor("out_d", [NTOK, D], F32, kind="ExternalOutput").ap()
    h1T_d = nc.dram_tensor("h1T_d", [128, 16, NTOK], BF16, kind="ExternalOutput").ap() if with_next else None
    scr = [nc.dram_tensor(f"scr_gate{i}", [16, 128], F32).ap() for i in range(nl)]
    ctx = Ctx(nc)
    P = ctx.P
    consts = load_consts(ctx, cst, 0)
    adaT = [ctx.tile(f"adaT{i}", consts["end"] + 256 * i, [128, 48], F32) for i in range(nl)]
    gate_bc = [ctx.tile(f"gate_bc{i}", adaT[-1].end + 8192 * i, [128, D], F32) for i in range(nl)]
    base = gate_bc[-1].end
    for l in range(nl):
        phase_ada(ctx, c16, ada_w[l], ada_b[l], base, adaT[l], scr[l], gate_bc[l])
        P.barrier()
    phase_outproj_ln(ctx, og_in, P.buf("og"), w_out, x_d, P.buf("x"), gate_bc[0], lng, lnb, out_d, P.buf("out"), consts, base,
                     ntok=NTOK, adaT_next=(adaT[1] if with_next else None), h1T_d=h1T_d,
                     h1buf=(P.buf("h1") if with_next else None))
    P.emit()
    return nc


def build_prog_C():
    nc = bass.Bass("TRN2", target_bir_lowering=False)
    I = lambda n, shp, dt=F32: nc.dram_tensor(n, shp, dt, kind="ExternalInput").ap()
    cst = I("consts", [128, NCONST], BF16); esel = I("esel", [32, 4096], BF16); m64 = I("m64", [128, 128])
    hT_d = I("hT_in", [128, 16, S], BF16); w1 = I("w1", [D, NH1 * 512])
    og_d = nc.dram_tensor("og_out", [NH1 * 128, S], BF16, kind="ExternalOutput").ap()
    ctx = Ctx(nc)
    P = ctx.P
    consts = load_consts(ctx, cst, 0)
    T = alloc_head_tiles(ctx, consts["end"])
    M = moba_alloc(ctx, esel, m64, T["end"])
    hbuf = P.buf("hT_in")
    ogbuf = P.buf("og_out")
    for h in range(NH1):
        inproj_head(ctx, T, w1, h * 512, hT_d, hbuf)
        moba_head(ctx, T, M, consts, og_d, h * 128, ogbuf)
    P.emit()
    return nc


def _f32(a):
    return np.ascontiguousarray(np.asarray(a), dtype=np.float32)


def kernel(x, c, ada_w, ada_b, ln_g, ln_b, even_w_in, even_w_out, ssm_a_re, ssm_a_im, ssm_log_dt,
           ssm_b_re, ssm_b_im, ssm_c_re, ssm_c_im, ssm_d, ssm_w_glu, odd_w_in, odd_w_out):
    x = _f32(x); c = _f32(c); ada_w = _f32(ada_w); ada_b = _f32(ada_b); ln_g = _f32(ln_g); ln_b = _f32(ln_b)
    w_in0 = _f32(even_w_in)[0]; w_out0 = _f32(even_w_out)[0]; w_in1 = _f32(odd_w_in)[0]; w_out1 = _f32(odd_w_out)[0]
    B = x.shape[0]
    ncores = 2 * B
    cores = list(range(ncores))
    consts_np = make_consts_np()
    esel_np, m64_np = make_moba_consts_np()
    HALF = S // 2
    inA = []
    chords = []
    for core in cores:
        b, j = divmod(core, 2)
        lay, chord = ssm_layout_np(_f32(ssm_a_re)[0], _f32(ssm_a_im)[0], _f32(ssm_log_dt)[0], _f32(ssm_b_re)[0], _f32(ssm_b_im)[0],
                                   _f32(ssm_c_re)[0], _f32(ssm_c_im)[0], _f32(ssm_d)[0], _f32(ssm_w_glu)[0], j)
        chords.append(chord)
        cols = []
        for h in range(NH0 * j, NH0 * j + NH0):
            for part in range(4):
                cols.append(np.arange(part * 1536 + h * 128, part * 1536 + (h + 1) * 128))
        cols.append(6144 + chord)
        cols.append(6144 + 512 + chord[:256])
        cols = np.concatenate(cols)
        m = {"x_b": x[b], "c16": c[b].reshape(16, 128), "ada_w": ada_w[0], "ada_b": ada_b[0].reshape(48, 128),
             "consts": consts_np, "w0": np.ascontiguousarray(w_in0[:, cols])}
        m.update(lay)
        inA.append(m)
    ncA = build_prog_A()
    resA = run_bass_kernel_spmd(ncA, inA, core_ids=cores)
    ogA = [np.asarray(r["og_out"]) for r in resA.results]
    inB = []
    rows0 = np.concatenate([np.concatenate([np.arange(768 * j, 768 * j + 768), 1536 + chords[j][:256]]) for j in range(2)])
    w_out0p = np.ascontiguousarray(w_out0[rows0])
    for core in cores:
        b, j = divmod(core, 2)
        ts = slice(HALF * j, HALF * (j + 1))
        og_full = np.ascontiguousarray(np.concatenate([ogA[2 * b][:, ts], ogA[2 * b + 1][:, ts]], axis=0))
        inB.append({"consts": consts_np, "og_in": og_full, "w_out": w_out0p, "x_d": np.ascontiguousarray(x[b, ts]),
                    "c16": c[b].reshape(16, 128), "ada_w": ada_w[0:2], "ada_b": ada_b[0:2].reshape(2, 48, 128),
                    "lng": ln_g[0], "lnb": ln_b[0]})
    ncB = build_prog_BD(True)
    resB = run_bass_kernel_spmd(ncB, inB, core_ids=cores)
    x1 = [np.asarray(r["out_d"]) for r in resB.results]
    h1T = [np.asarray(r["h1T_d"]) for r in resB.results]
    inC = []
    for core in cores:
        b, j = divmod(core, 2)
        cols = []
        for h in range(NH1 * j, NH1 * j + NH1):
            for part in range(4):
                cols.append(np.arange(part * 2048 + h * 128, part * 2048 + (h + 1) * 128))
        cols = np.concatenate(cols)
        hT_full = np.ascontiguousarray(np.concatenate([h1T[2 * b], h1T[2 * b + 1]], axis=2))
        inC.append({"consts": consts_np, "esel": esel_np, "m64": m64_np, "hT_in": hT_full,
                    "w1": np.ascontiguousarray(w_in1[:, cols])})
    ncC = build_prog_C()
    resC = run_bass_kernel_spmd(ncC, inC, core_ids=cores)
    ogC = [np.asarray(r["og_out"]) for r in resC.results]
    inD = []
    for core in cores:
        b, j = divmod(core, 2)
        ts = slice(HALF * j, HALF * (j + 1))
        og_full = np.ascontiguousarray(np.concatenate([ogC[2 * b][:, ts], ogC[2 * b + 1][:, ts]], axis=0))
        inD.append({"consts": consts_np, "og_in": og_full, "w_out": w_out1, "x_d": x1[core],
                    "c16": c[b].reshape(16, 128), "ada_w": ada_w[1:2], "ada_b": ada_b[1:2].reshape(1, 48, 128),
                    "lng": ln_g[1], "lnb": ln_b[1]})
    ncD = build_prog_BD(False)
    resD = run_bass_kernel_spmd(ncD, inD, core_ids=cores)
    out = np.empty((B, S, D), np.float32)
    for core in cores:
        b, j = divmod(core, 2)
        out[b, HALF * j:HALF * (j + 1)] = np.asarray(resD.results[core]["out_d"])
    return out
```

```python
import contextlib
import numpy as np
import concourse.bass as bass
import concourse.mybir as mybir
from concourse.bass_utils import run_bass_kernel_spmd

F32 = mybir.dt.float32


class _NOENV:
    get = staticmethod(lambda k, d=None: d)

BF16 = mybir.dt.bfloat16
AF = mybir.ActivationFunctionType
ALU = mybir.AluOpType
AX = mybir.AxisListType


class Buf:
    __slots__ = ("name", "w", "r", "frozen", "excl")

    def __init__(self, name, excl=False):
        self.name = name
        self.w = None
        self.r = {}
        self.frozen = False
        self.excl = excl


class Op:
    __slots__ = ("eng", "fn", "waits", "sig", "count", "idx", "dsem", "dval", "isdma")


ENGS = ("pe", "act", "dve", "pool", "sp")
NDMASEM = 24


class Prog:
    def __init__(self, nc):
        self.nc = nc
        self.ops = {e: [] for e in ENGS}
        self.waited = {e: {} for e in ENGS}
        self.dma_last = [None] * NDMASEM
        self.dma_cnt = [0] * NDMASEM
        self.dma_rr = 0
        self.nbuf = 0
        self.pending = {e: [] for e in ENGS}
        self.segs = [self.ops]
        self.seg_dma = []
        self.nidx = {e: 0 for e in ENGS}
        self.hooks = {}

    def new_segment(self):
        self.seg_dma.append(list(self.dma_cnt))
        self.ops = {e: [] for e in ENGS}
        self.segs.append(self.ops)

    def barrier(self):
        deps = [self.ops[e][-1] for e in ENGS if self.ops[e]]
        deps += [d for d in self.dma_last if d is not None]
        for e in ENGS:
            self.pending[e] = list(deps)

    def buf(self, name=None, excl=False):
        self.nbuf += 1
        return Buf(name or f"b{self.nbuf}", excl)

    def _dep(self, op, d):
        if d is None or d is op:
            return
        E = op.eng
        wt = self.waited[E]
        if d.isdma:
            key = ("d", d.dsem)
            if wt.get(key, 0) >= d.dval:
                return
            wt[key] = d.dval
            op.waits.append(d)
        else:
            X = d.eng
            if X == E and E == "pe":
                return
            if wt.get(X, -1) >= d.idx:
                return
            wt[X] = d.idx
            d.sig = True
            op.waits.append(d)

    def _record(self, eng, fn, reads, writes, isdma):
        xr = [b for b in reads if b.excl]
        if xr:
            reads = [b for b in reads if not b.excl]
            writes = list(writes) + [b for b in xr if b not in writes]
        op = Op()
        op.eng = eng
        op.fn = fn
        op.waits = []
        op.sig = False
        op.count = 0
        op.isdma = isdma
        op.dsem = None
        op.dval = 0
        op.idx = self.nidx[eng]
        self.nidx[eng] += 1
        if self.pending[eng]:
            for d in self.pending[eng]:
                self._dep(op, d)
            self.pending[eng] = []
        for b in reads:
            self._dep(op, b.w)
        for b in writes:
            self._dep(op, b.w)
            for r in b.r.values():
                self._dep(op, r)
        if isdma:
            k = self.dma_rr
            self.dma_rr = (k + 1) % NDMASEM
            self._dep(op, self.dma_last[k])
            self.dma_cnt[k] += 16
            op.dsem = k
            op.dval = self.dma_cnt[k]
            self.dma_last[k] = op
        for b in reads:
            if not b.frozen:
                key = (eng, op.idx) if isdma else eng
                b.r[key] = op
        for b in writes:
            b.w = op
            b.r = {}
        self.ops[eng].append(op)
        return op

    def op(self, eng, fn, reads=(), writes=()):
        return self._record(eng, fn, reads, writes, False)

    def dma(self, eng, fn, reads=(), writes=()):
        return self._record(eng, fn, reads, writes, True)

    def emit(self, final_waits=()):
        nc = self.nc
        self.seg_dma.append(list(self.dma_cnt))
        lasts_by_seg = []
        for seg in self.segs:
            lasts = []
            for e in ENGS:
                if e != "sp" and seg[e]:
                    lo = seg[e][-1]
                    if not lo.isdma:
                        lo.sig = True
                        lasts.append(lo)
            lasts_by_seg.append(lasts)
        for e in ENGS:
            c = 0
            for seg in self.segs:
                for op in seg[e]:
                    if op.sig and not op.isdma:
                        c += 1
                        op.count = c
        with contextlib.ExitStack() as st:
            esem = {e: st.enter_context(nc.semaphore(f"s_{e}")) for e in ENGS}
            dsem = [st.enter_context(nc.semaphore(f"s_d{k}")) for k in range(NDMASEM)]
            for si, seg in enumerate(self.segs):
                if si > 0:
                    nc.all_core_barrier()
                lasts = lasts_by_seg[si]
                dcnt = self.seg_dma[si]
                with nc.Block() as block:

                    def run(e, eng, seg=seg, lasts=lasts, dcnt=dcnt):
                        if e in self.hooks:
                            self.hooks[e](eng)
                        for op in seg[e]:
                            for d in op.waits:
                                if d.isdma:
                                    eng.wait_ge(dsem[d.dsem], d.dval)
                                else:
                                    eng.wait_ge(esem[d.eng], d.count)
                            ins = op.fn(eng)
                            if op.isdma:
                                ins.then_inc(dsem[op.dsem], 16)
                            elif op.sig:
                                ins.then_inc(esem[e], 1)
                        if e == "sp":
                            for lo in lasts:
                                eng.wait_ge(esem[lo.eng], lo.count)
                            for k in range(NDMASEM):
                                if dcnt[k]:
                                    eng.wait_ge(dsem[k], dcnt[k])

                    @block.tensor
                    def _(eng):
                        run("pe", eng)

                    @block.scalar
                    def _(eng):
                        run("act", eng)

                    @block.vector
                    def _(eng):
                        run("dve", eng)

                    @block.gpsimd
                    def _(eng):
                        run("pool", eng)

                    @block.sync
                    def _(eng):
                        run("sp", eng)

    def stats(self):
        return {e: sum(len(seg[e]) for seg in self.segs) for e in ENGS}


import os
import numpy as np
import ml_dtypes

SB_BASE = 16512
SB_SIZE = 204800
S = 8192
D = 2048
NT = S // 512
BIG = 30000.0
NCONST = 7 * 128
NROT = int(_NOENV.get('NROT', 2))


class Tile:
    def __init__(self, ctx, name, off, shape, dtype):
        off = (off + 31) // 32 * 32
        self.t = ctx.nc.alloc_sbuf_tensor_at(f"{name}_{ctx.uid()}", list(shape), dtype, offset=SB_BASE + off)
        self.b = ctx.P.buf(name)
        esz = 2 if dtype == BF16 else 4
        n = 1
        for s in shape[1:]:
            n *= s
        self.end = off + n * esz
        assert self.end <= SB_SIZE, (name, self.end)

    def __getitem__(self, k):
        return self.t[k]


class Ctx:
    def __init__(self, nc):
        self.nc = nc
        self.P = Prog(nc)
        self._uid = 0
        self.arena = nc.alloc_sbuf_tensor("arena", [128, SB_SIZE], mybir.dt.uint8)
        self.ps = []
        self.psb = []
        for k in range(8):
            t = nc.alloc_psum_tensor(f"psum{k}", [128, 512], F32)
            self.ps.append(t)
            self.psb.append(self.P.buf(f"psum{k}", excl=True))

    def uid(self):
        self._uid += 1
        return self._uid

    def tile(self, name, off, shape, dtype):
        return Tile(self, name, off, shape, dtype)

    def psbf(self, k):
        return self.ps[k][:].bitcast(BF16)


def load_consts(ctx, consts_bf, off):
    P = ctx.P
    t = ctx.tile("consts", off, [128, NCONST], BF16)
    P.dma("sp", lambda e: e.dma_start(out=t[:], in_=consts_bf), writes=[t.b])
    t.b.frozen = True
    c = {}
    for i, n in enumerate(["ident", "uinc", "lstr", "sbmask", "mbmask", "ones", "zeros"]):
        c[n] = t[:, i * 128:(i + 1) * 128]
    c["buf"] = t.b
    c["end"] = t.end
    return c


def make_consts_np():
    j = np.arange(128)[:, None]
    s = np.arange(128)[None, :]
    ident = (j == s).astype(np.float32)
    uinc = (j >= s).astype(np.float32)
    lstr = (j < s).astype(np.float32)
    sbmask = np.where(j < s, 0.0, -BIG)
    mbmask = np.where(j <= s, 0.0, -BIG)
    ones = np.ones((128, 128), np.float32)
    zeros = np.zeros((128, 128), np.float32)
    return np.concatenate([ident, uinc, lstr, sbmask, mbmask, ones, zeros], axis=1).astype(ml_dtypes.bfloat16)


def phase_ada(ctx, c16, ada_w_l, ada_b_l, off, adaT, scr_gate, gate_bc):
    nc, P = ctx.nc, ctx.P
    condT = ctx.tile("condT", off, [128, 16], F32)
    abT = ctx.tile("abT", condT.end, [128, 48], F32)
    slab = [ctx.tile(f"slab{i}", abT.end + i * 32768, [128, 16, 512], F32) for i in range(2)]
    P.dma("sp", lambda e: e.dma_start(out=condT[:], in_=c16.rearrange("k p -> p k"), allow_slow_non_contiguous=True),
          writes=[condT.b])
    P.dma("sp", lambda e: e.dma_start(out=abT[:], in_=ada_b_l.rearrange("k p -> p k"), allow_slow_non_contiguous=True),
          writes=[abT.b])
    P.op("act", lambda e: e.activation(out=condT[:], in_=condT[:], func=AF.Silu), reads=[condT.b], writes=[condT.b])
    pb = 7
    for jg in range(12):
        sl = slab[jg % 2]
        P.dma("sp" if jg % 2 == 0 else "act",
              lambda e, sl=sl, jg=jg: e.dma_start(
                  out=sl[:], in_=ada_w_l[:, jg * 512:(jg + 1) * 512].rearrange("(k p) f -> p k f", p=128)),
              writes=[sl.b])
        for jj in range(4):
            j = jg * 4 + jj
            for kc in range(16):
                P.op("pe", lambda e, sl=sl, jj=jj, kc=kc, j=j: e.matmul(
                    ctx.ps[pb][:, j:j + 1], lhsT=sl[:, kc, jj * 128:(jj + 1) * 128], rhs=condT[:, kc:kc + 1],
                    start=(kc == 0), stop=(kc == 15)),
                    reads=[sl.b, condT.b], writes=[ctx.psb[pb]])
    P.op("dve", lambda e: e.tensor_tensor(out=adaT[:], in0=ctx.ps[pb][:, 0:48], in1=abT[:], op=ALU.add),
         reads=[ctx.psb[pb], abT.b], writes=[adaT.b])
    P.op("dve", lambda e: e.tensor_scalar(out=adaT[:, 16:48], in0=adaT[:, 16:48], scalar1=1.0, scalar2=None, op0=ALU.add),
         reads=[adaT.b], writes=[adaT.b])
    gb = P.buf("scr_gate")
    P.dma("sp", lambda e: e.dma_start(out=scr_gate.rearrange("k p -> p k"), in_=adaT[:, 32:48],
                                      allow_slow_non_contiguous=True), reads=[adaT.b], writes=[gb])
    P.dma("sp", lambda e: e.dma_start(out=gate_bc[:], in_=scr_gate.rearrange("k p -> (k p)").partition_broadcast(128)),
          reads=[gb], writes=[gate_bc.b])


def phase_prepass(ctx, x_b, hT_d, adaT, consts, off, hbuf=None):
    nc, P = ctx.nc, ctx.P
    xt = [ctx.tile(f"xt{i}", off + i * 16384, [128, 4, 2048], BF16) for i in range(2)]
    ht = [ctx.tile(f"ht{i}", off + 32768 + i * 16384, [128, 16, 512], BF16) for i in range(2)]
    hb = hbuf or P.buf("hT_d")
    for tt in range(NT):
        x_ = xt[tt % 2]
        h_ = ht[tt % 2]
        P.dma("pool", lambda e, x_=x_, tt=tt: e.dma_start(
            out=x_[:], in_=x_b[tt * 512:(tt + 1) * 512, :].rearrange("(b p) f -> p b f", p=128)), writes=[x_.b])
        for kc in range(16):
            pb = kc % 2
            for blk in range(4):
                P.op("pe", lambda e, x_=x_, kc=kc, blk=blk, pb=pb: e.transpose(
                    ctx.psbf(pb)[:, blk * 128:(blk + 1) * 128], x_[:, blk, kc * 128:(kc + 1) * 128], consts["ident"]),
                    reads=[x_.b, consts["buf"]], writes=[ctx.psb[pb]])
            P.op("act", lambda e, h_=h_, kc=kc, pb=pb: e.activation(
                out=h_[:, kc, :], in_=ctx.psbf(pb)[:, 0:512], func=AF.Identity,
                scale=adaT[:, 16 + kc:17 + kc], bias=adaT[:, kc:kc + 1]),
                reads=[ctx.psb[pb], adaT.b], writes=[h_.b])
        P.dma("sp", lambda e, h_=h_, tt=tt: e.dma_start(out=hT_d[:, :, tt * 512:(tt + 1) * 512], in_=h_[:]),
              reads=[h_.b], writes=[hb])
    return hb


def alloc_head_tiles(ctx, off):
    T = {}
    T["qT"] = ctx.tile("qT", off, [128, S], BF16)
    T["kT"] = ctx.tile("kT", T["qT"].end, [128, S], BF16)
    T["v"] = ctx.tile("v", T["kT"].end, [128, S], BF16)
    T["sgT"] = ctx.tile("sgT", T["v"].end, [128, S], BF16)
    T["W"] = ctx.tile("W", T["sgT"].end, [128, 16, 512], BF16)
    T["ht"] = [ctx.tile(f"hti{i}", T["W"].end + i * 16384, [128, 16, 512], BF16) for i in range(2)]
    T["end"] = T["ht"][1].end
    return T


def inproj_head(ctx, T, w_d, col0, hT_d, hbuf, psbanks=(0, 1, 2, 3), gate_func=AF.Silu):
    P = ctx.P
    W = T["W"]
    P.dma("pool", lambda e: e.dma_start(out=W[:], in_=w_d[:, col0:col0 + 512].rearrange("(k p) f -> p k f", p=128)),
          writes=[W.b])
    nb = 0
    for tt in range(NT):
        ht = T["ht"][tt % 2]
        P.dma("sp", lambda e, ht=ht, tt=tt: e.dma_start(out=ht[:], in_=hT_d[:, :, tt * 512:(tt + 1) * 512]),
              reads=[hbuf], writes=[ht.b])
        ts = slice(tt * 512, (tt + 1) * 512)
        for ci, dst, fn in ((0, T["qT"], AF.Identity), (1, T["kT"], AF.Identity), (3, T["sgT"], gate_func)):
            pb = psbanks[nb % len(psbanks)]
            nb += 1
            for kc in range(16):
                P.op("pe", lambda e, pb=pb, kc=kc, ci=ci, ht=ht: e.matmul(
                    ctx.ps[pb][:, :], lhsT=W[:, kc, ci * 128:(ci + 1) * 128], rhs=ht[:, kc, :],
                    start=(kc == 0), stop=(kc == 15)), reads=[W.b, ht.b], writes=[ctx.psb[pb]])
            P.op("act", lambda e, pb=pb, dst=dst, ts=ts, fn=fn: e.activation(out=dst[:, ts], in_=ctx.ps[pb][:, :], func=fn),
                 reads=[ctx.psb[pb]], writes=[dst.b])
        pb = psbanks[nb % len(psbanks)]
        nb += 1
        for blk in range(4):
            for kc in range(16):
                P.op("pe", lambda e, pb=pb, kc=kc, blk=blk, ht=ht: e.matmul(
                    ctx.ps[pb][:, blk * 128:(blk + 1) * 128], lhsT=ht[:, kc, blk * 128:(blk + 1) * 128],
                    rhs=W[:, kc, 256:384], start=(kc == 0), stop=(kc == 15)),
                    reads=[W.b, ht.b], writes=[ctx.psb[pb]])
        P.op("dve", lambda e, pb=pb, ts=ts: e.tensor_copy(out=T["v"][:, ts], in_=ctx.ps[pb][:, :]),
             reads=[ctx.psb[pb]], writes=[T["v"].b])


def alloc_attn_tmp(ctx, off):
    A = {}
    o = off
    for nm, dt, sz in (("e", F32, 2048), ("sp", BF16, 1024), ("arg", F32, 2048), ("w", BF16, 1024), ("og", BF16, 1024)):
        A[nm] = []
        for i in range(NROT):
            A[nm].append(ctx.tile(f"{nm}{i}", o, [128, 512], dt))
            o += sz
    A["end"] = o
    return A


def sb_attention_head(ctx, T, A, consts, og_d, row0, ogbuf, og_fn=None):
    P = ctx.P
    scale = 128 ** -0.5
    qT, kT, v, sgT = T["qT"], T["kT"], T["v"], T["sgT"]
    cb = consts["buf"]
    steps = []
    import os
    for G in range(int(_NOENV.get("NG", NT))):
        for kb in range(4 * G + 3, -1, -1):
            steps.append((G, kb))
    steps = steps[int(_NOENV.get("SKIP", 0)):int(_NOENV.get("NSTEPS", len(steps)))]
    n = len(steps)

    def geom(i):
        G, kb = steps[i]
        diag = kb >= 4 * G
        c0 = (kb - 4 * G) * 128 if diag else 0
        return G, kb, diag, c0, i % NROT, 3 + G % 2, 5 + G % 2

    def Zmm(i):
        G, kb, diag, c0, zb, Cb, Ob = geom(i)
        ks = slice(kb * 128, (kb + 1) * 128)
        q0 = G * 512
        z = ctx.ps[zb]
        if kb == 4 * G + 3 or i == 0:
            for bk in (Cb, Ob):
                P.op("pe", lambda e, bk=bk: e.matmul(ctx.ps[bk][:, 0:512], lhsT=consts["zeros"], rhs=qT[:, q0:q0 + 512],
                                                     start=True, stop=False, skip_group_check=True),
                     reads=[cb, qT.b], writes=[ctx.psb[bk]])
        P.op("pe", lambda e: e.matmul(z[:, c0:512], lhsT=kT[:, ks], rhs=qT[:, q0 + c0:q0 + 512],
                                      start=True, stop=not diag), reads=[kT.b, qT.b], writes=[ctx.psb[zb]])
        if diag:
            P.op("pe", lambda e: e.matmul(z[:, c0:c0 + 128], lhsT=consts["ident"], rhs=consts["sbmask"],
                                          start=False, stop=True), reads=[cb], writes=[ctx.psb[zb]])

    def Zact(i):
        G, kb, diag, c0, zb, Cb, Ob = geom(i)
        e_, sp_ = A["e"][zb], A["sp"][zb]
        X = _NOENV.get("X%d" % i, "")
        if "e" not in X:
          P.op("act", lambda e: e.activation(out=e_[:, c0:512], in_=ctx.ps[zb][:, c0:512], func=AF.Exp, scale=scale),
             reads=[ctx.psb[zb]], writes=[e_.b])
        if "l" not in X:
          P.op("act", lambda e: e.activation(out=sp_[:, c0:512], in_=e_[:, c0:512], func=(AF.Identity if _NOENV.get("NOLN") else AF.Ln), bias=1.0),
             reads=[e_.b], writes=[sp_.b])
        arg_ = A["arg"][zb]
        if "s" not in X:
          P.op("dve", lambda e: e.tensor_scalar(out=arg_[:, c0:512], in0=ctx.ps[zb][:, c0:512], scalar1=scale, scalar2=None,
                                              op0=ALU.mult), reads=[ctx.psb[zb]], writes=[arg_.b])

    def U(i):
        G, kb, diag, c0, zb, Cb, Ob = geom(i)
        sp_, arg_ = A["sp"][zb], A["arg"][zb]
        C = ctx.ps[Cb]
        P.op("pe", lambda e: e.matmul(C[:, c0:512], lhsT=consts["uinc"], rhs=sp_[:, c0:512],
                                      start=False, stop=False, skip_group_check=True),
             reads=[cb, sp_.b], writes=[ctx.psb[Cb]])
        P.op("dve", lambda e: e.tensor_tensor(out=arg_[:, c0:512], in0=arg_[:, c0:512], in1=C[:, c0:512],
                                              op=ALU.subtract), reads=[arg_.b, ctx.psb[Cb]], writes=[arg_.b])

    def Lmm(i):
        G, kb, diag, c0, zb, Cb, Ob = geom(i)
        sp_ = A["sp"][zb]
        C = ctx.ps[Cb]
        if kb > 0:
            P.op("pe", lambda e: e.matmul(C[:, c0:512], lhsT=consts["lstr"], rhs=sp_[:, c0:512],
                                          start=False, stop=False, skip_group_check=True),
                 reads=[cb, sp_.b], writes=[ctx.psb[Cb]])

    def Wact(i):
        G, kb, diag, c0, zb, Cb, Ob = geom(i)
        arg_, w_ = A["arg"][zb], A["w"][zb]
        P.op("act", lambda e: e.activation(out=w_[:, c0:512], in_=arg_[:, c0:512], func=AF.Exp),
             reads=[arg_.b], writes=[w_.b])

    def PV(i):
        G, kb, diag, c0, zb, Cb, Ob = geom(i)
        w_ = A["w"][zb]
        O = ctx.ps[Ob]
        ks = slice(kb * 128, (kb + 1) * 128)
        last = (kb == 0) or (i == n - 1)
        P.op("pe", lambda e: e.matmul(O[:, c0:512], lhsT=v[:, ks], rhs=w_[:, c0:512],
                                      start=False, stop=False, skip_group_check=True),
             reads=[v.b, w_.b], writes=[ctx.psb[Ob]])
        if last:
            og_ = A["og"][G % 2]
            P.op("dve", lambda e: e.tensor_tensor(out=og_[:, :], in0=O[:, :], in1=sgT[:, G * 512:(G + 1) * 512],
                                                  op=ALU.mult), reads=[ctx.psb[Ob], sgT.b], writes=[og_.b])
            P.dma("sp", lambda e: e.dma_start(out=(og_fn(e, row0, 128, slice(G * 512, (G + 1) * 512)) if og_fn else
                                                   og_d[row0:row0 + 128, G * 512:(G + 1) * 512]), in_=og_[:, :]),
                  reads=[og_.b], writes=[ogbuf])

    if _NOENV.get("SEQ"):
        dis = _NOENV.get("DIS", "")
        for i in range(n):
            Zmm(i)
            if "a" not in dis: Zact(i)
            if "u" not in dis:
                U(i)
                Lmm(i)
            if "w" not in dis: Wact(i)
            if "p" not in dis or i == n - 1: PV(i)
        return
    Zmm(0)
    Zact(0)
    for i in range(n):
        if i + 1 < n:
            Zmm(i + 1)
        U(i)
        if i >= 1:
            PV(i - 1)
        if i + 1 < n:
            Zact(i + 1)
        Lmm(i)
        Wact(i)
    PV(n - 1)


TWO_PI = 2.0 * np.pi
LT = 512


def ssm_layout_np(a_re, a_im, log_dt, b_re, b_im, c_re, c_im, d_skip, w_glu, own):
    order = np.concatenate([np.arange(16 * own, 16 * own + 16), np.arange(16 * (1 - own), 16 * (1 - own) + 16)])
    chord = (order[:, None] * 16 + np.arange(16)[None, :]).reshape(-1)
    f = lambda x: np.ascontiguousarray(x, dtype=np.float32)
    A = lambda x: f(x[order].reshape(16, 2, 64).transpose(1, 2, 0).reshape(128, 16))
    out = {}
    out["s_are"] = A(a_re)
    out["s_aim"] = A(a_im)
    out["s_ldt"] = f(np.repeat(log_dt[order].reshape(16, 2, 1), 64, axis=2).transpose(1, 2, 0).reshape(128, 16))
    Bf = lambda x: f(x[order].reshape(16, 2, 64, 16).transpose(1, 2, 0, 3).reshape(128, 16, 16))
    out["s_bre"] = Bf(b_re)
    out["s_bim"] = Bf(b_im)
    def Cf(x):
        xo = x[order].reshape(16, 2, 16, 64)
        r = np.zeros((2, 16, 16, 128), np.float32)
        r[0, :, :, 0:64] = xo[:, 0].transpose(1, 0, 2)
        r[1, :, :, 64:128] = xo[:, 1].transpose(1, 0, 2)
        return r
    out["s_cre"] = Cf(c_re)
    out["s_cim"] = Cf(c_im)
    out["s_d"] = f(d_skip[chord].reshape(4, 128).T)
    wg = w_glu[chord][:, chord[:256]]
    out["s_wglu"] = f(wg)
    E = np.zeros((2, 16, 4, 128), np.float32)
    for g2 in range(2):
        for o in range(16):
            for k in range(4):
                E[g2, o, k, k * 32 + g2 * 16 + o] = 1.0
    out["s_E"] = E
    out["s_tpos"] = f(np.broadcast_to(np.arange(LT + 1, dtype=np.float32), (128, LT + 1)))
    return out, chord


def ssm_setup(ctx, prm, consts, off):
    P = ctx.P
    Sx = {}
    o = off
    def T(name, shape, dt):
        nonlocal o
        t = ctx.tile(name, o, shape, dt)
        o = t.end
        return t
    cosT = T("cosT", [128, 16, LT], BF16)
    sinT = T("sinT", [128, 16, LT], BF16)
    Bre = T("Bbd_re", [128, 16, 128], BF16)
    Bim = T("Bbd_im", [128, 16, 128], BF16)
    Cre = T("Cbd_re", [128, 16, 128], BF16)
    Cimn = T("Cbd_imn", [128, 16, 128], BF16)
    rr = T("r", [128, 16], F32)
    cL = T("cL", [128, 16], F32)
    sL = T("sL", [128, 16], F32)
    Dt = T("Dsk", [128, 4], F32)
    Wg = T("Wglu", [128, 4, 256], BF16)
    persist_end = o
    are = T("are", [128, 16], F32); aim = T("aim", [128, 16], F32); dt_ = T("dt", [128, 16], F32)
    th = T("theta", [128, 16], F32)
    tpos = T("tpos", [128, LT + 1], F32)
    ph = T("phase", [128, LT], F32)
    braw = T("braw_re", [128, 16, 16], F32); biaw = T("braw_im", [128, 16, 16], F32)
    bbr = T("bbar_re", [128, 16, 16], F32); bbi = T("bbar_im", [128, 16, 16], F32)
    tmp = [T(f"st{i}", [128, 16], F32) for i in range(8)]
    tb = [T(f"stb{i}", [128, 16, 16], F32) for i in range(2)]
    Z = [T(f"Z{i}", [128, 128], BF16) for i in range(2)]
    Cl = {}
    for nm in ("cre", "cim"):
        for g2 in range(2):
            Cl[nm, g2] = T(f"Cl_{nm}{g2}", [16, 16, 128], BF16)
    Eb = [T(f"E{g2}", [16, 4, 128], BF16) for g2 in range(2)]

    ld = lambda t, src, q="sp": P.dma(q, lambda e: e.dma_start(out=t[:], in_=src), writes=[t.b])
    ld(are, prm["s_are"]); ld(aim, prm["s_aim"]); ld(dt_, prm["s_ldt"]); ld(tpos, prm["s_tpos"])
    ld(braw, prm["s_bre"]); ld(biaw, prm["s_bim"]); ld(Dt, prm["s_d"])
    for nm in ("cre", "cim"):
        for g2 in range(2):
            ld(Cl[nm, g2], prm["s_" + nm][g2], "pool")
    for g2 in range(2):
        ld(Eb[g2], prm["s_E"][g2], "pool")
    P.dma("pool", lambda e: e.dma_start(out=Wg[:], in_=prm["s_wglu"].rearrange("(c p) n -> p c n", p=128)), writes=[Wg.b])

    V = lambda fn, reads, writes: P.op("dve", fn, reads=[t.b for t in reads], writes=[t.b for t in writes])
    P.op("act", lambda e: e.activation(out=dt_[:], in_=dt_[:], func=AF.Exp), reads=[dt_.b], writes=[dt_.b])
    V(lambda e: e.tensor_tensor(out=th[:], in0=aim[:], in1=dt_[:], op=ALU.mult), [aim, dt_], [th])
    V(lambda e: e.tensor_tensor(out=rr[:], in0=are[:], in1=dt_[:], op=ALU.mult), [are, dt_], [rr])
    P.op("act", lambda e: e.activation(out=rr[:], in_=rr[:], func=AF.Exp), reads=[rr.b], writes=[rr.b])

    def sincos(out_sin, out_cos, gp, tcols, n):
        for dst, shift in ((out_sin, 0.0), (out_cos, np.pi / 2)):
            V(lambda e: e.tensor_scalar(out=ph[:, 0:n], in0=tpos[:, tcols], scalar1=th[:, gp:gp + 1], scalar2=None,
                                        op0=ALU.mult), [tpos, th], [ph])
            V(lambda e, shift=shift: e.tensor_scalar(out=ph[:, 0:n], in0=ph[:, 0:n], scalar1=float(shift + np.pi),
                                                     scalar2=TWO_PI, op0=ALU.add, op1=ALU.mod), [ph], [ph])
            P.op("act", lambda e, dst=dst: e.activation(out=dst, in_=ph[:, 0:n], func=AF.Sin, bias=negpi[:, 0:1]),
                 reads=[ph.b, negpi.b], writes=[dstb[0]])

    negpi = tmp[7]
    V(lambda e: e.memset(negpi[:], -float(np.pi)), [], [negpi])
    dstb = [None]
    s1, c1, kre, kim, den, t0, t1 = tmp[0], tmp[1], tmp[2], tmp[3], tmp[4], tmp[5], tmp[6]
    MAGIC = 12582912.0
    INV2PI = float(1.0 / TWO_PI)

    def range_sin(dst_ap, dst_buf, x, xk, n_shape):
        V(lambda e: e.tensor_scalar(out=xk, in0=x, scalar1=INV2PI, scalar2=MAGIC, op0=ALU.mult, op1=ALU.add), n_shape[0], n_shape[1])
        V(lambda e: e.tensor_scalar(out=xk, in0=xk, scalar1=-MAGIC, scalar2=None, op0=ALU.add), n_shape[1], n_shape[1])
        V(lambda e: e.scalar_tensor_tensor(out=x, in0=xk, scalar=-TWO_PI, in1=x, op0=ALU.mult, op1=ALU.add),
          n_shape[0] + n_shape[1], n_shape[0])
        P.op("act", lambda e: e.activation(out=dst_ap, in_=x, func=AF.Sin), reads=[t.b for t in n_shape[0]], writes=[dst_buf])

    for dst, shift in ((s1, 0.0), (c1, np.pi / 2)):
        V(lambda e, shift=shift: e.tensor_scalar(out=t0[:], in0=th[:], scalar1=float(shift), scalar2=None, op0=ALU.add), [th], [t0])
        range_sin(dst[:], dst.b, t0[:], t1[:], ([t0], [t1]))
    V(lambda e: e.tensor_tensor(out=c1[:], in0=c1[:], in1=rr[:], op=ALU.mult), [c1, rr], [c1])
    V(lambda e: e.tensor_scalar(out=c1[:], in0=c1[:], scalar1=-1.0, scalar2=None, op0=ALU.add), [c1], [c1])
    V(lambda e: e.tensor_tensor(out=s1[:], in0=s1[:], in1=rr[:], op=ALU.mult), [s1, rr], [s1])
    V(lambda e: e.tensor_tensor(out=den[:], in0=are[:], in1=are[:], op=ALU.mult), [are], [den])
    V(lambda e: e.tensor_tensor(out=t0[:], in0=aim[:], in1=aim[:], op=ALU.mult), [aim], [t0])
    V(lambda e: e.tensor_tensor(out=den[:], in0=den[:], in1=t0[:], op=ALU.add), [den, t0], [den])
    V(lambda e: e.reciprocal(out=den[:], in_=den[:]), [den], [den])
    V(lambda e: e.tensor_tensor(out=kre[:], in0=c1[:], in1=are[:], op=ALU.mult), [c1, are], [kre])
    V(lambda e: e.tensor_tensor(out=t0[:], in0=s1[:], in1=aim[:], op=ALU.mult), [s1, aim], [t0])
    V(lambda e: e.tensor_tensor(out=kre[:], in0=kre[:], in1=t0[:], op=ALU.add), [kre, t0], [kre])
    V(lambda e: e.tensor_tensor(out=kre[:], in0=kre[:], in1=den[:], op=ALU.mult), [kre, den], [kre])
    V(lambda e: e.tensor_tensor(out=kim[:], in0=s1[:], in1=are[:], op=ALU.mult), [s1, are], [kim])
    V(lambda e: e.tensor_tensor(out=t0[:], in0=c1[:], in1=aim[:], op=ALU.mult), [c1, aim], [t0])
    V(lambda e: e.tensor_tensor(out=kim[:], in0=kim[:], in1=t0[:], op=ALU.subtract), [kim, t0], [kim])
    V(lambda e: e.tensor_tensor(out=kim[:], in0=kim[:], in1=den[:], op=ALU.mult), [kim, den], [kim])
    kb = lambda k: k[:].unsqueeze(2).to_broadcast([128, 16, 16])
    V(lambda e: e.tensor_tensor(out=bbr[:], in0=braw[:], in1=kb(kre), op=ALU.mult), [braw, kre], [bbr])
    V(lambda e: e.tensor_tensor(out=tb[0][:], in0=biaw[:], in1=kb(kim), op=ALU.mult), [biaw, kim], [tb[0]])
    V(lambda e: e.tensor_tensor(out=bbr[:], in0=bbr[:], in1=tb[0][:], op=ALU.subtract), [bbr, tb[0]], [bbr])
    V(lambda e: e.tensor_tensor(out=bbi[:], in0=biaw[:], in1=kb(kre), op=ALU.mult), [biaw, kre], [bbi])
    V(lambda e: e.tensor_tensor(out=tb[1][:], in0=braw[:], in1=kb(kim), op=ALU.mult), [braw, kim], [tb[1]])
    V(lambda e: e.tensor_tensor(out=bbi[:], in0=bbi[:], in1=tb[1][:], op=ALU.add), [bbi, tb[1]], [bbi])
    n = 0
    for src, dstT in ((bbr, Bre), (bbi, Bim)):
        for gp in range(16):
            k = gp % 4
            z = Z[n % 2]
            pb = 6 + n % 2
            n += 1
            V(lambda e, z=z: e.memset(z[:], 0.0), [], [z])
            V(lambda e, z=z, src=src, gp=gp, k=k: e.tensor_copy(out=z[0:64, k * 32:k * 32 + 16], in_=src[0:64, gp, :]),
              [src], [z])
            V(lambda e, z=z, src=src, gp=gp, k=k: e.tensor_copy(out=z[64:128, k * 32 + 16:k * 32 + 32], in_=src[64:128, gp, :]),
              [src], [z])
            P.op("pe", lambda e, z=z, pb=pb: e.transpose(ctx.psbf(pb)[:, 0:128], z[:], consts["ident"]),
                 reads=[z.b, consts["buf"]], writes=[ctx.psb[pb]])
            V2 = P.op("act", lambda e, dstT=dstT, gp=gp, pb=pb: e.activation(out=dstT[:, gp, :], in_=ctx.psbf(pb)[:, 0:128],
                                                                           func=AF.Identity),
                      reads=[ctx.psb[pb]], writes=[dstT.b])
    for nm, dstT, sc in (("cre", Cre, 1.0), ("cim", Cimn, -1.0)):
        for gp in range(16):
            k = gp % 4
            pb = 6 + n % 2
            n += 1
            for g2 in range(2):
                P.op("pe", lambda e, nm=nm, g2=g2, gp=gp, k=k, pb=pb: e.matmul(
                    ctx.ps[pb][:, 0:128], lhsT=Cl[nm, g2][:, gp, :], rhs=Eb[g2][:, k, :], start=(g2 == 0), stop=(g2 == 1)),
                    reads=[Cl[nm, g2].b, Eb[g2].b], writes=[ctx.psb[pb]])
            P.op("act", lambda e, dstT=dstT, gp=gp, pb=pb, sc=sc: e.activation(out=dstT[:, gp, :], in_=ctx.ps[pb][:, 0:128],
                                                                              func=AF.Identity, scale=sc),
                 reads=[ctx.psb[pb]], writes=[dstT.b])
    ph2 = T("phase2", [128, LT], F32)
    for gp in range(16):
        for dst, shift in ((sinT, 0.0), (cosT, np.pi / 2)):
            V(lambda e, gp=gp, shift=shift: e.tensor_scalar(out=ph[:, 0:LT], in0=tpos[:, 0:LT], scalar1=th[:, gp:gp + 1],
                                                          scalar2=float(shift), op0=ALU.mult, op1=ALU.add), [tpos, th], [ph])
            range_sin(dst[:, gp, :], dst.b, ph[:, 0:LT], ph2[:, 0:LT], ([ph], [ph2]))
    for dst, shift in ((sL, 0.0), (cL, np.pi / 2)):
        V(lambda e, shift=shift: e.tensor_scalar(out=t0[:], in0=th[:], scalar1=float(LT), scalar2=float(shift), op0=ALU.mult,
                                                 op1=ALU.add), [th], [t0])
        range_sin(dst[:], dst.b, t0[:], t1[:], ([t0], [t1]))
    Sx.update(cosT=cosT, sinT=sinT, Bre=Bre, Bim=Bim, Cre=Cre, Cimn=Cimn, r=rr, cL=cL, sL=sL, D=Dt, Wg=Wg, end=persist_end)
    return Sx


def ssm_pass(ctx, Sx, consts, w_d, col0, hT_d, hbuf, og_d, row0, ogbuf, off, ntiles=NT, og_fn=None):
    P = ctx.P
    o = off
    def T(name, shape, dt):
        nonlocal o
        t = ctx.tile(name, o, shape, dt)
        o = t.end
        return t
    W = T("Wssm", [128, 16, 768], BF16)
    ht = [T(f"sht{i}", [128, 16, 512], BF16) for i in range(2)]
    uT = [T(f"uT{i}", [128, 4, 512], BF16) for i in range(2)]
    sgs = [T(f"sgs{i}", [128, 2, 512], BF16) for i in range(2)]
    st = []
    for k in range(2):
        st.append({n: T(f"{n}{k}", [128, 512], F32) for n in ("t1", "t2", "Xre", "Xim", "Gre", "Gim", "p1", "p2")})
        st[k]["hre"] = T(f"hre{k}", [128, 512], BF16)
        st[k]["him"] = T(f"him{k}", [128, 512], BF16)
    ini_re = T("ini_re", [128, 16], F32)
    ini_im = T("ini_im", [128, 16], F32)
    tc = T("tcarry", [128, 2], F32)
    yd = T("yd", [128, 4, 512], F32)
    tg = T("tg", [128, 4, 512], F32)
    ygb = T("ygb", [128, 4, 512], BF16)
    sgl = [T(f"sgl{i}", [128, 512], F32) for i in range(2)]
    ob = [T(f"ob{i}", [128, 512], BF16) for i in range(2)]
    print("ssm sbuf end", o)
    cosT, sinT, r, cL, sL = Sx["cosT"], Sx["sinT"], Sx["r"], Sx["cL"], Sx["sL"]
    V = lambda fn, reads, writes: P.op("dve", fn, reads=reads, writes=writes)
    G = lambda fn, reads, writes: P.op("pool", fn, reads=reads, writes=writes)
    P.dma("pool", lambda e: e.dma_start(out=W[:], in_=w_d[:, col0:col0 + 768].rearrange("(k p) f -> p k f", p=128)),
          writes=[W.b])
    V(lambda e: e.memset(ini_re[:], 0.0), [], [ini_re.b])
    V(lambda e: e.memset(ini_im[:], 0.0), [], [ini_im.b])
    nb = 0
    for tt in range(ntiles):
        h_ = ht[tt % 2]
        u_ = uT[tt % 2]
        g_ = sgs[tt % 2]
        ts = slice(tt * 512, (tt + 1) * 512)
        P.dma("sp", lambda e, h_=h_, tt=tt: e.dma_start(out=h_[:], in_=hT_d[:, :, tt * 512:(tt + 1) * 512]),
              reads=[hbuf], writes=[h_.b])
        for fc in range(6):
            pb = nb % 2
            nb += 1
            for kc in range(16):
                P.op("pe", lambda e, pb=pb, kc=kc, fc=fc, h_=h_: e.matmul(
                    ctx.ps[pb][:, :], lhsT=W[:, kc, fc * 128:(fc + 1) * 128], rhs=h_[:, kc, :],
                    start=(kc == 0), stop=(kc == 15)), reads=[W.b, h_.b], writes=[ctx.psb[pb]])
            if fc < 4:
                P.op("act", lambda e, pb=pb, u_=u_, fc=fc: e.activation(out=u_[:, fc, :], in_=ctx.ps[pb][:, :], func=AF.Identity),
                     reads=[ctx.psb[pb]], writes=[u_.b])
            else:
                P.op("act", lambda e, pb=pb, g_=g_, fc=fc: e.activation(out=g_[:, fc - 4, :], in_=ctx.ps[pb][:, :], func=AF.Silu),
                     reads=[ctx.psb[pb]], writes=[g_.b])
        for gp in range(16):
            c = gp // 4
            k = gp % 2
            S_ = st[k]
            ba, bb = (2, 3) if k == 0 else (4, 5)
            P.op("pe", lambda e, gp=gp, c=c, ba=ba, u_=u_: e.matmul(ctx.ps[ba][:, :], lhsT=Sx["Bre"][:, gp, :], rhs=u_[:, c, :],
                                                                start=True, stop=True),
                 reads=[Sx["Bre"].b, u_.b], writes=[ctx.psb[ba]])
            P.op("pe", lambda e, gp=gp, c=c, bb=bb, u_=u_: e.matmul(ctx.ps[bb][:, :], lhsT=Sx["Bim"][:, gp, :], rhs=u_[:, c, :],
                                                                start=True, stop=True),
                 reads=[Sx["Bim"].b, u_.b], writes=[ctx.psb[bb]])
            co, si = cosT[:, gp, :], sinT[:, gp, :]
            V(lambda e, S_=S_, ba=ba, co=co: e.tensor_tensor(out=S_["t1"][:], in0=ctx.ps[ba][:, :], in1=co, op=ALU.mult),
              [ctx.psb[ba], cosT.b], [S_["t1"].b])
            V(lambda e, S_=S_, ba=ba, si=si: e.tensor_tensor(out=S_["Xim"][:], in0=ctx.ps[ba][:, :], in1=si, op=ALU.mult),
              [ctx.psb[ba], sinT.b], [S_["Xim"].b])
            V(lambda e, S_=S_, bb=bb, si=si: e.tensor_tensor(out=S_["t2"][:], in0=ctx.ps[bb][:, :], in1=si, op=ALU.mult),
              [ctx.psb[bb], sinT.b], [S_["t2"].b])
            V(lambda e, S_=S_, bb=bb, co=co: e.tensor_tensor(out=S_["Xre"][:], in0=ctx.ps[bb][:, :], in1=co, op=ALU.mult),
              [ctx.psb[bb], cosT.b], [S_["Xre"].b])
            G(lambda e, S_=S_: e.tensor_tensor(out=S_["Xim"][:], in0=S_["Xre"][:], in1=S_["Xim"][:], op=ALU.subtract),
              [S_["Xre"].b, S_["Xim"].b], [S_["Xim"].b])
            G(lambda e, S_=S_: e.tensor_tensor(out=S_["Xre"][:], in0=S_["t1"][:], in1=S_["t2"][:], op=ALU.add),
              [S_["t1"].b, S_["t2"].b], [S_["Xre"].b])
            rb = r[:, gp:gp + 1].to_broadcast([128, 512])
            V(lambda e, S_=S_, rb=rb, gp=gp: e.tensor_tensor_scan(out=S_["Gre"][:], data0=rb, data1=S_["Xre"][:],
                                                               initial=ini_re[:, gp:gp + 1], op0=ALU.mult, op1=ALU.add),
              [r.b, S_["Xre"].b, ini_re.b], [S_["Gre"].b])
            V(lambda e, S_=S_, rb=rb, gp=gp: e.tensor_tensor_scan(out=S_["Gim"][:], data0=rb, data1=S_["Xim"][:],
                                                               initial=ini_im[:, gp:gp + 1], op0=ALU.mult, op1=ALU.add),
              [r.b, S_["Xim"].b, ini_im.b], [S_["Gim"].b])
            gl_re, gl_im = S_["Gre"][:, 511:512], S_["Gim"][:, 511:512]
            V(lambda e, gp=gp, gl_im=gl_im: e.tensor_tensor(out=tc[:, 0:1], in0=gl_im, in1=sL[:, gp:gp + 1], op=ALU.mult),
              [S_["Gim"].b, sL.b], [tc.b])
            V(lambda e, gp=gp, gl_im=gl_im: e.tensor_tensor(out=tc[:, 1:2], in0=gl_im, in1=cL[:, gp:gp + 1], op=ALU.mult),
              [S_["Gim"].b, cL.b], [tc.b])
            V(lambda e, gp=gp, gl_re=gl_re: e.scalar_tensor_tensor(out=ini_re[:, gp:gp + 1], in0=gl_re, scalar=cL[:, gp:gp + 1],
                                                                  in1=tc[:, 0:1], op0=ALU.mult, op1=ALU.subtract),
              [S_["Gre"].b, cL.b, tc.b], [ini_re.b])
            V(lambda e, gp=gp, gl_re=gl_re: e.scalar_tensor_tensor(out=ini_im[:, gp:gp + 1], in0=gl_re, scalar=sL[:, gp:gp + 1],
                                                                  in1=tc[:, 1:2], op0=ALU.mult, op1=ALU.add),
              [S_["Gre"].b, sL.b, tc.b], [ini_im.b])
            G(lambda e, S_=S_, co=co: e.tensor_tensor(out=S_["p1"][:], in0=S_["Gre"][:], in1=co, op=ALU.mult),
              [S_["Gre"].b, cosT.b], [S_["p1"].b])
            G(lambda e, S_=S_, si=si: e.tensor_tensor(out=S_["p2"][:], in0=S_["Gim"][:], in1=si, op=ALU.mult),
              [S_["Gim"].b, sinT.b], [S_["p2"].b])
            G(lambda e, S_=S_: e.tensor_tensor(out=S_["hre"][:], in0=S_["p1"][:], in1=S_["p2"][:], op=ALU.subtract),
              [S_["p1"].b, S_["p2"].b], [S_["hre"].b])
            G(lambda e, S_=S_, si=si: e.tensor_tensor(out=S_["p1"][:], in0=S_["Gre"][:], in1=si, op=ALU.mult),
              [S_["Gre"].b, sinT.b], [S_["p1"].b])
            G(lambda e, S_=S_, co=co: e.tensor_tensor(out=S_["p2"][:], in0=S_["Gim"][:], in1=co, op=ALU.mult),
              [S_["Gim"].b, cosT.b], [S_["p2"].b])
            G(lambda e, S_=S_: e.tensor_tensor(out=S_["him"][:], in0=S_["p1"][:], in1=S_["p2"][:], op=ALU.add),
              [S_["p1"].b, S_["p2"].b], [S_["him"].b])
            P.op("pe", lambda e, gp=gp, S_=S_: e.matmul(ctx.ps[6][:, :], lhsT=Sx["Cre"][:, gp, :], rhs=S_["hre"][:],
                                                    start=(gp % 4 == 0), stop=False),
                 reads=[Sx["Cre"].b, S_["hre"].b], writes=[ctx.psb[6]])
            P.op("pe", lambda e, gp=gp, S_=S_: e.matmul(ctx.ps[6][:, :], lhsT=Sx["Cimn"][:, gp, :], rhs=S_["him"][:],
                                                    start=False, stop=(gp % 4 == 3)),
                 reads=[Sx["Cimn"].b, S_["him"].b], writes=[ctx.psb[6]])
            if gp % 4 == 3:
                V(lambda e, c=c, u_=u_: e.scalar_tensor_tensor(out=yd[:, c, :], in0=u_[:, c, :], scalar=Sx["D"][:, c:c + 1],
                                                            in1=ctx.ps[6][:, :], op0=ALU.mult, op1=ALU.add),
                  [u_.b, Sx["D"].b, ctx.psb[6]], [yd.b])
        ydf = yd[:].rearrange("p c t -> p (c t)")
        tgf = tg[:].rearrange("p c t -> p (c t)")
        ygf = ygb[:].rearrange("p c t -> p (c t)")
        G(lambda e: e.tensor_tensor(out=tgf, in0=ydf, in1=ydf, op=ALU.mult), [yd.b], [tg.b])
        G(lambda e: e.tensor_scalar(out=tgf, in0=tgf, scalar1=0.044715, scalar2=1.0, op0=ALU.mult, op1=ALU.add), [tg.b], [tg.b])
        G(lambda e: e.tensor_tensor(out=tgf, in0=tgf, in1=ydf, op=ALU.mult), [tg.b, yd.b], [tg.b])
        P.op("act", lambda e: e.activation(out=tgf, in_=tgf, func=AF.Sigmoid, scale=1.5957691216057308),
             reads=[tg.b], writes=[tg.b])
        V(lambda e: e.tensor_tensor(out=ygf, in0=tgf, in1=ydf, op=ALU.mult), [tg.b, yd.b], [ygb.b])
        for oc in range(2):
            for ci in range(4):
                P.op("pe", lambda e, oc=oc, ci=ci: e.matmul(ctx.ps[7][:, :], lhsT=Sx["Wg"][:, ci, oc * 128:(oc + 1) * 128],
                                                         rhs=ygb[:, ci, :], start=(ci == 0), stop=(ci == 3)),
                     reads=[Sx["Wg"].b, ygb.b], writes=[ctx.psb[7]])
            sg_ = sgl[oc]
            o_ = ob[oc]
            P.op("act", lambda e, sg_=sg_: e.activation(out=sg_[:], in_=ctx.ps[7][:, :], func=AF.Sigmoid),
                 reads=[ctx.psb[7]], writes=[sg_.b])
            V(lambda e, sg_=sg_, oc=oc: e.tensor_tensor(out=sg_[:], in0=sg_[:], in1=ygb[:, oc, :], op=ALU.mult),
              [sg_.b, ygb.b], [sg_.b])
            V(lambda e, sg_=sg_, o_=o_, oc=oc, g_=g_: e.tensor_tensor(out=o_[:], in0=sg_[:], in1=g_[:, oc, :], op=ALU.mult),
              [sg_.b, g_.b], [o_.b])
            P.dma("sp", lambda e, o_=o_, oc=oc, ts=ts: e.dma_start(
                out=(og_fn(e, row0 + oc * 128, 128, ts) if og_fn else og_d[row0 + oc * 128:row0 + (oc + 1) * 128, ts]), in_=o_[:]),
                  reads=[o_.b], writes=[ogbuf])


DN_ALPHA = 4.0 ** 0.25
LN_EPS = 1e-5


def phase_outproj_ln(ctx, og_d, ogbuf, w_out_d, x_d, xbuf, gate_bc, lng_d, lnb_d, out_d, outbuf, consts, off,
                     ntok=4096, adaT_next=None, h1T_d=None, h1buf=None, tok0_h1=0, ogtile_fn=None, h1_fn=None):
    P = ctx.P
    o = off
    def T(name, shape, dt):
        nonlocal o
        t = ctx.tile(name, o, shape, dt)
        o = t.end
        return t
    Wo = T("Wo", [128, 16, 2048], BF16)
    lng = T("lng", [128, 2048], F32)
    lnb = T("lnb", [128, 2048], F32)
    ogt = [T(f"ogt{i}", [128, 16, 512], BF16) for i in range(2)]
    xt = [T(f"xres{i}", [128, 2048], F32) for i in range(2)]
    z = T("zres", [128, 2048], F32)
    xo = [T(f"xo{i}", [128, 2048], F32) for i in range(2)]
    stats = T("bnst", [128, 4, 6], F32)
    mv = T("bnmv", [128, 2], F32)
    rstd = T("rstd", [128, 1], F32)
    if adaT_next is not None:
        xb = T("xob", [128, 2048], BF16)
        h1 = T("h1t", [128, 16, 512], BF16)
    print("phaseB sbuf end", o)
    for half in range(2):
        P.dma("pool", lambda e, half=half: e.dma_start(
            out=Wo[:, half * 8:(half + 1) * 8, :],
            in_=w_out_d[half * 1024:(half + 1) * 1024, :].rearrange("(k p) f -> p k f", p=128)), writes=[Wo.b])
    P.dma("sp", lambda e: e.dma_start(out=lng[:], in_=lng_d.partition_broadcast(128)), writes=[lng.b])
    P.dma("sp", lambda e: e.dma_start(out=lnb[:], in_=lnb_d.partition_broadcast(128)), writes=[lnb.b])
    V = lambda fn, reads, writes: P.op("dve", fn, reads=reads, writes=writes)
    G = lambda fn, reads, writes: P.op("pool", fn, reads=reads, writes=writes)
    nblk = ntok // 128
    for blk in range(nblk):
        tt, bi = divmod(blk, 4)
        og_ = ogt[tt % 2]
        if bi == 0:
            P.dma("sp", lambda e, og_=og_, tt=tt: e.dma_start(
                out=og_[:], in_=(ogtile_fn(e, tt) if ogtile_fn else
                                 og_d[:, tt * 512:(tt + 1) * 512].rearrange("(k p) t -> p k t", p=128))),
                reads=[ogbuf], writes=[og_.b])
        x_ = xt[blk % 2]
        xo_ = xo[blk % 2]
        P.dma("act", lambda e, x_=x_, blk=blk: e.dma_start(out=x_[:], in_=x_d[blk * 128:(blk + 1) * 128, :]),
              reads=[xbuf], writes=[x_.b])
        for nc_ in range(4):
            for fc in range(16):
                P.op("pe", lambda e, nc_=nc_, fc=fc, og_=og_, bi=bi: e.matmul(
                    ctx.ps[nc_][:, :], lhsT=og_[:, fc, bi * 128:(bi + 1) * 128], rhs=Wo[:, fc, nc_ * 512:(nc_ + 1) * 512],
                    start=(fc == 0), stop=(fc == 15)), reads=[og_.b, Wo.b], writes=[ctx.psb[nc_]])
            cs = slice(nc_ * 512, (nc_ + 1) * 512)
            V(lambda e, nc_=nc_, cs=cs: e.tensor_tensor(out=z[:, cs], in0=ctx.ps[nc_][:, :], in1=gate_bc[:, cs], op=ALU.mult),
              [ctx.psb[nc_], gate_bc.b], [z.b])
        V(lambda e, x_=x_: e.scalar_tensor_tensor(out=z[:], in0=x_[:], scalar=DN_ALPHA, in1=z[:], op0=ALU.mult, op1=ALU.add),
          [x_.b, z.b], [z.b])
        for c4 in range(4):
            V(lambda e, c4=c4: e.bn_stats(out=stats[:, c4, :], in_=z[:, c4 * 512:(c4 + 1) * 512]), [z.b], [stats.b])
        V(lambda e: e.bn_aggr(out=mv[:], in_=stats[:].rearrange("p c s -> p (c s)")), [stats.b], [mv.b])
        V(lambda e: e.tensor_scalar(out=rstd[:], in0=mv[:, 1:2], scalar1=LN_EPS, scalar2=None, op0=ALU.add), [mv.b], [rstd.b])
        P.op("act", lambda e: e.activation(out=rstd[:], in_=rstd[:], func=AF.Sqrt), reads=[rstd.b], writes=[rstd.b])
        V(lambda e: e.reciprocal(out=rstd[:], in_=rstd[:]), [rstd.b], [rstd.b])
        V(lambda e: e.tensor_scalar(out=z[:], in0=z[:], scalar1=mv[:, 0:1], scalar2=rstd[:, 0:1], op0=ALU.subtract, op1=ALU.mult),
          [z.b, mv.b, rstd.b], [z.b])
        G(lambda e, xo_=xo_: e.tensor_tensor(out=xo_[:], in0=z[:], in1=lng[:], op=ALU.mult), [z.b, lng.b], [xo_.b])
        G(lambda e, xo_=xo_: e.tensor_tensor(out=xo_[:], in0=xo_[:], in1=lnb[:], op=ALU.add), [xo_.b, lnb.b], [xo_.b])
        P.dma("sp", lambda e, xo_=xo_, blk=blk: e.dma_start(out=out_d[blk * 128:(blk + 1) * 128, :], in_=xo_[:]),
              reads=[xo_.b], writes=[outbuf])
        if adaT_next is not None:
            P.op("act", lambda e, xo_=xo_: e.activation(out=xb[:], in_=xo_[:], func=AF.Identity), reads=[xo_.b], writes=[xb.b])
            for kc in range(16):
                pb = 4 + kc % 4
                P.op("pe", lambda e, kc=kc, pb=pb: e.transpose(ctx.psbf(pb)[:, 0:128], xb[:, kc * 128:(kc + 1) * 128],
                                                             consts["ident"]),
                     reads=[xb.b, consts["buf"]], writes=[ctx.psb[pb]])
                P.op("act", lambda e, kc=kc, pb=pb, bi=bi: e.activation(
                    out=h1[:, kc, bi * 128:(bi + 1) * 128], in_=ctx.psbf(pb)[:, 0:128], func=AF.Identity,
                    scale=adaT_next[:, 16 + kc:17 + kc], bias=adaT_next[:, kc:kc + 1]),
                    reads=[ctx.psb[pb], adaT_next.b], writes=[h1.b])
            if bi == 3:
                P.dma("sp", lambda e, tt=tt: e.dma_start(
                    out=(h1_fn(e, tt) if h1_fn else h1T_d[:, :, tok0_h1 + tt * 512:tok0_h1 + (tt + 1) * 512]), in_=h1[:]),
                    reads=[h1.b], writes=[h1buf])


NEG30 = -1.0e30


def make_moba_consts_np():
    esel = np.zeros((32, 32, 128), np.float32)
    for n in range(32):
        esel[n, n, :] = 1.0
    m = np.zeros((128, 128), np.float32)
    m[:, 32:64] = NEG30
    m[:, 64 + 32:128] = -BIG
    return esel.reshape(32, 32 * 128).astype(ml_dtypes.bfloat16), m


def moba_alloc(ctx, esel_d, m64_d, off):
    P = ctx.P
    o = off
    def T(name, shape, dt):
        nonlocal o
        t = ctx.tile(name, o, shape, dt)
        o = t.end
        return t
    M = {}
    M["esel"] = T("esel", [32, 32 * 128], BF16)
    M["m64"] = T("m64", [128, 128], F32)
    M["MbT"] = T("MbT", [32, S], BF16)
    M["km"] = T("km", [128, 32], F32)
    M["kmh"] = T("kmh", [128, 32], BF16)
    M["kml"] = T("kml", [128, 32], BF16)
    M["gm"] = T("gm", [128, 16, 32], F32)
    M["top8"] = T("top8", [128, 16, 8], F32)
    M["biasb"] = T("biasb", [128, 16, 32], BF16)
    M["pT"] = [T(f"pT{i}", [128, 512], BF16) for i in range(2)]
    M["psum"] = [T(f"psacc{i}", [128, 512], F32) for i in range(2)]
    M["rec"] = [T(f"rec{i}", [128, 512], F32) for i in range(2)]
    M["og"] = [T(f"mog{i}", [128, 512], BF16) for i in range(2)]
    M["onesf"] = T("onesf", [128, 128], F32)
    M["end"] = o
    P.dma("sp", lambda e: e.dma_start(out=M["esel"][:], in_=esel_d), writes=[M["esel"].b])
    P.dma("sp", lambda e: e.dma_start(out=M["m64"][:], in_=m64_d), writes=[M["m64"].b])
    P.op("dve", lambda e: e.memset(M["onesf"][:], 1.0), writes=[M["onesf"].b])
    return M


def moba_head(ctx, T, M, consts, og_d, row0, ogbuf, og_fn=None):
    P = ctx.P
    scale = 128 ** -0.5
    qT, kT, v, sgT = T["qT"], T["kT"], T["v"], T["sgT"]
    cb = consts["buf"]
    V = lambda fn, reads, writes: P.op("dve", fn, reads=reads, writes=writes)
    km, kmh, kml, gm, top8, biasb, MbT, m64 = M["km"], M["kmh"], M["kml"], M["gm"], M["top8"], M["biasb"], M["MbT"], M["m64"]
    V(lambda e: e.tensor_reduce(out=km[:], in_=kT[:, :].rearrange("p (n k) -> p n k", k=256), axis=AX.X, op=ALU.add),
      [kT.b], [km.b])
    V(lambda e: e.tensor_scalar(out=km[:], in0=km[:], scalar1=1.0 / 256.0, scalar2=None, op0=ALU.mult), [km.b], [km.b])
    V(lambda e: e.tensor_copy(out=kmh[:], in_=km[:]), [km.b], [kmh.b])
    V(lambda e: e.tensor_tensor(out=km[:], in0=km[:], in1=kmh[:], op=ALU.subtract), [km.b, kmh.b], [km.b])
    V(lambda e: e.tensor_copy(out=kml[:], in_=km[:]), [km.b], [kml.b])
    for q16 in range(4):
        for j in range(16):
            qb = q16 * 16 + j
            qs = slice(qb * 128, (qb + 1) * 128)
            P.op("pe", lambda e, j=j, qs=qs: e.matmul(ctx.ps[4][:, j * 32:(j + 1) * 32], lhsT=qT[:, qs], rhs=kmh[:],
                                                      start=True, stop=False), reads=[qT.b, kmh.b], writes=[ctx.psb[4]])
            P.op("pe", lambda e, j=j, qs=qs: e.matmul(ctx.ps[4][:, j * 32:(j + 1) * 32], lhsT=qT[:, qs], rhs=kml[:],
                                                      start=False, stop=True), reads=[qT.b, kml.b], writes=[ctx.psb[4]])
        for j in range(16):
            qb = q16 * 16 + j
            nb = qb // 2
            V(lambda e, j=j, nb=nb: e.tensor_tensor(out=gm[:, j, :], in0=ctx.ps[4][:, j * 32:(j + 1) * 32],
                                                    in1=m64[:, 32 - nb:64 - nb], op=ALU.add),
              [ctx.psb[4], m64.b], [gm.b])
            V(lambda e, j=j: e.max(out=top8[:, j, :], in_=gm[:, j, :]), [gm.b], [top8.b])
            V(lambda e, j=j: e.tensor_scalar(out=gm[:, j, :], in0=gm[:, j, :], scalar1=top8[:, j, 2:3], scalar2=-BIG,
                                             op0=ALU.is_lt, op1=ALU.mult), [gm.b, top8.b], [gm.b])
            V(lambda e, j=j, nb=nb: e.tensor_tensor(out=biasb[:, j, :], in0=gm[:, j, :], in1=m64[:, 64 + 32 - nb:128 - nb],
                                                    op=ALU.add), [gm.b, m64.b], [biasb.b])
        for j in range(16):
            tbk = 5 if j < 8 else 7
            P.op("pe", lambda e, j=j, tbk=tbk: e.transpose(ctx.psbf(tbk)[0:32, (j % 8) * 128:(j % 8 + 1) * 128], biasb[:, j, :],
                                                         consts["ident"]),
                 reads=[biasb.b, cb], writes=[ctx.psb[tbk]])
        for hh in range(2):
            tbk = 5 if hh == 0 else 7
            P.op("act", lambda e, q16=q16, hh=hh, tbk=tbk: e.activation(
                out=MbT[:, q16 * 2048 + hh * 1024:q16 * 2048 + (hh + 1) * 1024], in_=ctx.psbf(tbk)[0:32, 0:1024],
                func=AF.Identity), reads=[ctx.psb[tbk]], writes=[MbT.b])
    steps = []
    for G in range(int(_NOENV.get("NG", NT))):
        for kb in range(0, 4 * G + 4):
            steps.append((G, kb))
    n = len(steps)

    def geom(i):
        G, kb = steps[i]
        r = kb - 4 * G
        c0 = 0 if r <= 0 else r * 128
        s0 = 0 if r < 0 else (256 if r in (0, 1) else None)
        return G, kb, r, c0, s0, (0, 1, 5)[i % 3], 2 + G % 2

    def Zmm(i):
        G, kb, r, c0, s0, zb, Ob = geom(i)
        ks = slice(kb * 128, (kb + 1) * 128)
        q0 = G * 512
        z = ctx.ps[zb]
        if kb == 0:
            P.op("pe", lambda e: e.matmul(ctx.ps[Ob][:, 0:512], lhsT=consts["zeros"], rhs=qT[:, q0:q0 + 512],
                                          start=True, stop=False, skip_group_check=True), reads=[cb, qT.b], writes=[ctx.psb[Ob]])
        P.op("pe", lambda e: e.matmul(z[:, c0:512], lhsT=kT[:, ks], rhs=qT[:, q0 + c0:q0 + 512], start=True,
                                      stop=(s0 is None and r < 0)), reads=[kT.b, qT.b], writes=[ctx.psb[zb]])
        if s0 is not None:
            nblk = kb // 2
            P.op("pe", lambda e: e.matmul(z[:, s0:512], lhsT=M["esel"][:, nblk * 128:(nblk + 1) * 128],
                                          rhs=MbT[:, q0 + s0:q0 + 512], start=False, stop=(r < 0)),
                 reads=[M["esel"].b, MbT.b], writes=[ctx.psb[zb]])
        if r >= 0:
            P.op("pe", lambda e: e.matmul(z[:, c0:c0 + 128], lhsT=consts["ident"], rhs=consts["mbmask"], start=False, stop=True),
                 reads=[cb], writes=[ctx.psb[zb]])

    def Pact(i):
        G, kb, r, c0, s0, zb, Ob = geom(i)
        p_ = M["pT"][i % 2]
        P.op("act", lambda e: e.activation(out=p_[:, c0:512], in_=ctx.ps[zb][:, c0:512], func=AF.Exp, scale=scale),
             reads=[ctx.psb[zb]], writes=[p_.b])

    def PV(i):
        G, kb, r, c0, s0, zb, Ob = geom(i)
        p_ = M["pT"][i % 2]
        acc = M["psum"][G % 2]
        ks = slice(kb * 128, (kb + 1) * 128)
        P.op("pe", lambda e: e.matmul(ctx.ps[Ob][:, c0:512], lhsT=v[:, ks], rhs=p_[:, c0:512], start=False, stop=False,
                                      skip_group_check=True), reads=[v.b, p_.b], writes=[ctx.psb[Ob]])
        if kb == 0:
            V(lambda e: e.tensor_copy(out=acc[:], in_=p_[:]), [p_.b], [acc.b])
        else:
            V(lambda e: e.tensor_tensor(out=acc[:, c0:512], in0=acc[:, c0:512], in1=p_[:, c0:512], op=ALU.add),
              [acc.b, p_.b], [acc.b])
        if kb == 4 * G + 3:
            rec_, og_ = M["rec"][G % 2], M["og"][G % 2]
            P.op("pe", lambda e: e.matmul(ctx.ps[6][:, :], lhsT=M["onesf"][:], rhs=acc[:], start=True, stop=True),
                 reads=[M["onesf"].b, acc.b], writes=[ctx.psb[6]])
            V(lambda e: e.reciprocal(out=rec_[:], in_=ctx.ps[6][:, :]), [ctx.psb[6]], [rec_.b])
            V(lambda e: e.tensor_tensor(out=rec_[:], in0=ctx.ps[Ob][:, :], in1=rec_[:], op=ALU.mult), [ctx.psb[Ob], rec_.b], [rec_.b])
            V(lambda e: e.tensor_tensor(out=og_[:], in0=rec_[:], in1=sgT[:, G * 512:(G + 1) * 512], op=ALU.mult),
              [rec_.b, sgT.b], [og_.b])
            P.dma("sp", lambda e: e.dma_start(out=(og_fn(e, row0, 128, slice(G * 512, (G + 1) * 512)) if og_fn else
                                                   og_d[row0:row0 + 128, G * 512:(G + 1) * 512]), in_=og_[:]),
                  reads=[og_.b], writes=[ogbuf])

    Zmm(0)
    if n > 1:
        Zmm(1)
    for i in range(n):
        if i + 2 < n:
            Zmm(i + 2)
        Pact(i)
        PV(i)


SSM_SHAPES = {"s_are": [128, 16], "s_aim": [128, 16], "s_ldt": [128, 16], "s_bre": [128, 16, 16], "s_bim": [128, 16, 16],
              "s_cre": [2, 16, 16, 128], "s_cim": [2, 16, 16, 128], "s_d": [128, 4], "s_wglu": [512, 256],
              "s_E": [2, 16, 4, 128], "s_tpos": [128, LT + 1]}
NH0 = 6
NH1 = 8
HALF = S // 2


def build_prog_fused():
    nc = bass.Bass("TRN2", target_bir_lowering=False, num_devices=8)
    I = lambda n, shp, dt=F32: nc.dram_tensor(n, shp, dt, kind="ExternalInput").ap()
    x_b = I("x_b", [S, D]); x_own = I("x_own", [HALF, D]); c16 = I("c16", [16, 128])
    ada_w = I("ada_w", [2, D, 3 * D]); ada_b = I("ada_b", [2, 48, 128])
    cst = I("consts", [128, NCONST], BF16); w0 = I("w0", [D, NH0 * 512 + 768])
    prm = {k: I(k, v) for k, v in SSM_SHAPES.items()}
    w_out0 = I("w_out0", [D, D]); w_out1 = I("w_out1", [D, D]); w1 = I("w1", [D, NH1 * 512])
    lng = I("lng", [2, D]); lnb = I("lnb", [2, D])
    esel = I("esel", [32, 4096], BF16); m64 = I("m64", [128, 128])
    out_d = nc.dram_tensor("out_d", [HALF, D], F32, kind="ExternalOutput").ap()
    hT_d = nc.dram_tensor("hT_scr", [128, 16, S], BF16).ap()
    x1_d = nc.dram_tensor("x1_scr", [HALF, D], F32).ap()
    scr = [nc.dram_tensor(f"scr_gate{i}", [16, 128], F32).ap() for i in range(2)]
    og_sh = nc.dram_tensor("og_sh", [2, 1024, S], BF16, addr_space="Shared").ap()
    h1_sh = nc.dram_tensor("h1_sh", [128, 16, S], BF16, addr_space="Shared").ap()
    ctx = Ctx(nc)
    P = ctx.P
    og_loc = nc.dram_tensor("og_loc", [1024, S], BF16).ap()
    ogin_loc = nc.dram_tensor("ogin_loc", [D, HALF], BF16).ap()
    h1_loc = nc.dram_tensor("h1_loc", [128, 16, HALF], BF16).ap()
    dyn = {}

    def sp_init(e):
        jv = e.snap(e.partition_id() % 2, min_val=0, max_val=1)
        tok0 = e.snap(jv * HALF, min_val=0, max_val=HALF)
        dyn["jv"] = jv
        dyn["tok0"] = tok0

    P.hooks["sp"] = sp_init
    ogsh_buf = P.buf("og_sh")

    def publish_og(ogbuf):
        P.dma("sp", lambda e: e.dma_start(out=og_sh[bass.ds(dyn["jv"], 1), :, :].rearrange("o r c -> (o r) c"), in_=og_loc),
              reads=[ogbuf], writes=[ogsh_buf])

    def fetch_og(oginbuf):
        for s_ in range(2):
            P.dma("sp", lambda e, s_=s_: e.dma_start(out=ogin_loc[s_ * 1024:(s_ + 1) * 1024, :],
                                                   in_=og_sh[s_, :, bass.ds(dyn["tok0"], HALF)]),
                  reads=[ogsh_buf], writes=[oginbuf])

    consts = load_consts(ctx, cst, 0)
    adaT = [ctx.tile(f"adaT{i}", consts["end"] + 256 * i, [128, 48], F32) for i in range(2)]
    gate_bc = [ctx.tile(f"gate_bc{i}", adaT[1].end + 8192 * i, [128, D], F32) for i in range(2)]
    base = gate_bc[1].end
    for l in range(2):
        phase_ada(ctx, c16, ada_w[l], ada_b[l], base, adaT[l], scr[l], gate_bc[l])
        P.barrier()
    hbuf = phase_prepass(ctx, x_b, hT_d, adaT[0], consts, base)
    P.barrier()
    T = alloc_head_tiles(ctx, base)
    A = alloc_attn_tmp(ctx, T["end"])
    ogbuf = P.buf("og_loc")
    for h in range(NH0):
        inproj_head(ctx, T, w0, h * 512, hT_d, hbuf)
        sb_attention_head(ctx, T, A, consts, og_loc, h * 128, ogbuf)
    P.barrier()
    Sx = ssm_setup(ctx, prm, consts, base)
    P.barrier()
    ssm_pass(ctx, Sx, consts, w0, NH0 * 512, hT_d, hbuf, og_loc, NH0 * 128, ogbuf, Sx["end"])
    publish_og(ogbuf)
    P.new_segment()
    x1buf = P.buf("x1_scr")
    h1buf = P.buf("h1_loc")
    h1sh_buf = P.buf("h1_sh")
    oginbuf = P.buf("ogin_loc")
    fetch_og(oginbuf)
    phase_outproj_ln(ctx, ogin_loc, oginbuf, w_out0, x_own, P.buf("x_own"), gate_bc[0], lng[0], lnb[0], x1_d, x1buf, consts, base,
                     ntok=HALF, adaT_next=adaT[1], h1T_d=h1_loc, h1buf=h1buf)
    P.dma("sp", lambda e: e.dma_start(out=h1_sh[:, :, bass.ds(dyn["tok0"], HALF)], in_=h1_loc), reads=[h1buf], writes=[h1sh_buf])
    P.new_segment()
    T = alloc_head_tiles(ctx, base)
    M = moba_alloc(ctx, esel, m64, T["end"])
    for h in range(NH1):
        inproj_head(ctx, T, w1, h * 512, h1_sh, h1sh_buf)
        moba_head(ctx, T, M, consts, og_loc, h * 128, ogbuf)
    publish_og(ogbuf)
    P.new_segment()
    fetch_og(oginbuf)
    phase_outproj_ln(ctx, ogin_loc, oginbuf, w_out1, x1_d, x1buf, gate_bc[1], lng[1], lnb[1], out_d, P.buf("out"), consts, base,
                     ntok=HALF)
    P.emit()
    return nc


def _f32(a):
    return np.ascontiguousarray(np.asarray(a), dtype=np.float32)


def kernel(x, c, ada_w, ada_b, ln_g, ln_b, even_w_in, even_w_out, ssm_a_re, ssm_a_im, ssm_log_dt,
           ssm_b_re, ssm_b_im, ssm_c_re, ssm_c_im, ssm_d, ssm_w_glu, odd_w_in, odd_w_out):
    x = _f32(x); c = _f32(c); ada_w = _f32(ada_w); ada_b = _f32(ada_b); ln_g = _f32(ln_g); ln_b = _f32(ln_b)
    w_in0 = _f32(even_w_in)[0]; w_out0 = _f32(even_w_out)[0]; w_in1 = _f32(odd_w_in)[0]; w_out1 = _f32(odd_w_out)[0]
    B = x.shape[0]
    ncores = 2 * B
    cores = list(range(ncores))
    consts_np = make_consts_np()
    esel_np, m64_np = make_moba_consts_np()
    sp = [_f32(a)[0] for a in (ssm_a_re, ssm_a_im, ssm_log_dt, ssm_b_re, ssm_b_im, ssm_c_re, ssm_c_im, ssm_d, ssm_w_glu)]
    lays = [ssm_layout_np(*sp, j) for j in range(2)]
    rows0 = np.concatenate([np.concatenate([np.arange(768 * j, 768 * j + 768), 1536 + lays[j][1][:256]]) for j in range(2)])
    w_out0p = np.ascontiguousarray(w_out0[rows0])
    ada_b2 = np.ascontiguousarray(ada_b.reshape(2, 48, 128))
    w0s, w1s = [], []
    for j in range(2):
        lay, chord = lays[j]
        cols = []
        for h in range(NH0 * j, NH0 * j + NH0):
            for part in range(4):
                cols.append(np.arange(part * 1536 + h * 128, part * 1536 + (h + 1) * 128))
        cols.append(6144 + chord)
        cols.append(6144 + 512 + chord[:256])
        w0s.append(np.ascontiguousarray(w_in0[:, np.concatenate(cols)]))
        cols = []
        for h in range(NH1 * j, NH1 * j + NH1):
            for part in range(4):
                cols.append(np.arange(part * 2048 + h * 128, part * 2048 + (h + 1) * 128))
        w1s.append(np.ascontiguousarray(w_in1[:, np.concatenate(cols)]))
    in_maps = []
    for core in cores:
        b, j = divmod(core, 2)
        m = {"x_b": x[b], "x_own": np.ascontiguousarray(x[b, HALF * j:HALF * (j + 1)]), "c16": c[b].reshape(16, 128),
             "ada_w": ada_w, "ada_b": ada_b2, "consts": consts_np, "w0": w0s[j], "w_out0": w_out0p, "w_out1": w_out1,
             "w1": w1s[j], "lng": ln_g, "lnb": ln_b, "esel": esel_np, "m64": m64_np}
        m.update(lays[j][0])
        in_maps.append(m)
    nc = build_prog_fused()
    res = run_bass_kernel_spmd(nc, in_maps, core_ids=cores)
    out = np.empty((B, S, D), np.float32)
    for core in cores:
        b, j = divmod(core, 2)
        out[b, HALF * j:HALF * (j + 1)] = np.asarray(res.results[core]["out_d"])
    return out
```
